# Optimizing a Trainium2 kernel written in Bass

```python
import math
import jax
import jax.numpy as jnp
from jax import lax
import numpy as np

D_MODEL = 1024
BATCH = 2
SEQ = 8192
DEPTH = 4

N_MIXERS = 4
HEAD_DIM = 64
RMS_EPS = 1e-6
NEG_INF = -1e30
TINY = 1e-30
Q_BLOCK = 128

A_HEADS = D_MODEL // HEAD_DIM
A_KV_HEADS = A_HEADS // 4
A_GROUP = A_HEADS // A_KV_HEADS
A_CMP_BLOCK = 32
A_CMP_STRIDE = 16
A_CMP_HIDDEN = 4 * HEAD_DIM
A_SEL_BLOCK = 64
A_SEL_TOPK = 16
A_WINDOW = 512
A_Q_BLOCK = 64
A_FORCE_BONUS = 1e3
A_SPLITS = (A_HEADS * HEAD_DIM,) + (A_KV_HEADS * HEAD_DIM,) * 6 + (3 * A_HEADS, D_MODEL)
A_IN = sum(A_SPLITS)

B_HEADS = D_MODEL // (2 * HEAD_DIM)
B_IN = 4 * D_MODEL

C_PATTERNS = ((128, 1), (512, 4), (2048, 16))
C_GROUPS = 3
C_HEADS = 4
C_V_DIM = D_MODEL // C_HEADS
C_Q_BLOCK = 64
C_IN = 2 * C_GROUPS * C_HEADS * HEAD_DIM + 2 * D_MODEL

D_HEADS = D_MODEL // HEAD_DIM
D_IN = 4 * D_MODEL

kernel_name = 'hybrid_nsa_diff_dilated_stickbreak_trunk'


def rms_norm(x, gain):
    xf = x.astype(jnp.float32)
    y = xf * lax.rsqrt(jnp.mean(xf * xf, axis=-1, keepdims=True) + RMS_EPS)
    return (y * gain.astype(jnp.float32)).astype(x.dtype)


def alibi_slopes(n):
    return jnp.asarray(np.array([2.0 ** (-8.0 * (i + 1) / n) for i in range(n)], np.float32))


def split_cols(a, sizes):
    return jnp.split(a, [int(c) for c in np.cumsum(sizes)[:-1]], axis=-1)


def masked_softmax(s, mask):
    s = jnp.where(mask, s, NEG_INF)
    e = jnp.where(mask, jnp.exp(s - jnp.max(s, axis=-1, keepdims=True)), 0.0)
    return e / jnp.maximum(jnp.sum(e, axis=-1, keepdims=True), TINY)


def unblock(o):
    nq, b, h, t, d = o.shape
    return o.transpose(1, 0, 3, 2, 4).reshape(b, nq * t, h * d)


def nsa_mixer(u, w_in, w_out, pos_k, pos_v, w1_k, w2_k, w1_v, w2_v):
    B, S, _ = u.shape
    H, Hk, G, dh = A_HEADS, A_KV_HEADS, A_GROUP, HEAD_DIM
    L, st, SB, W, TQ = A_CMP_BLOCK, A_CMP_STRIDE, A_SEL_BLOCK, A_WINDOW, A_Q_BLOCK
    f32 = jnp.float32
    q, kc, vc, ks, vs, kw, vw, gl, z = split_cols(u @ w_in, A_SPLITS)
    q = q.reshape(B, S, Hk, G, dh).transpose(0, 2, 3, 1, 4).astype(f32) * dh ** -0.5

    def heads(a):
        return a.reshape(B, S, Hk, dh).transpose(0, 2, 1, 3).astype(f32)

    kc, vc, ks, vs, kw, vw = (heads(a) for a in (kc, vc, ks, vs, kw, vw))

    n_cmp = (S - L) // st + 1
    cidx = st * np.arange(n_cmp)[:, None] + np.arange(L)[None, :]
    cmp_end = jnp.asarray(cidx[:, -1])

    def compress(a, pos, w1, w2):
        blk = (a[:, :, cidx] + pos).reshape(B, Hk, n_cmp, L * dh)
        return jax.nn.gelu(blk @ w1) @ w2

    kcmp = compress(kc, pos_k, w1_k, w2_k).astype(f32)
    vcmp = compress(vc, pos_v, w1_v, w2_v).astype(f32)

    n_sel = S // SB
    ov = np.zeros((n_cmp, n_sel), np.float32)
    np.add.at(ov, (np.repeat(np.arange(n_cmp), L), (cidx // SB).ravel()), 1.0 / L)
    ov = jnp.asarray(ov)
    ks_blk = ks.reshape(B, Hk, n_sel, SB, dh)
    vs_blk = vs.reshape(B, Hk, n_sel, SB, dh)
    topk = min(A_SEL_TOPK, n_sel)
    bi = jnp.arange(B)[:, None, None, None]
    hi = jnp.arange(Hk)[None, :, None, None]

    kw_p = jnp.pad(kw, ((0, 0), (0, 0), (W, 0), (0, 0)))
    vw_p = jnp.pad(vw, ((0, 0), (0, 0), (W, 0), (0, 0)))
    slopes = alibi_slopes(H).reshape(1, Hk, G, 1, 1)

    def block(i):
        q0 = i * TQ
        t = q0 + jnp.arange(TQ)
        qb = lax.dynamic_slice_in_dim(q, q0, TQ, axis=3)
        dist = t[:, None] - cmp_end[None, :]
        p_c = masked_softmax(jnp.einsum('bhgqd,bhnd->bhgqn', qb, kcmp) - slopes * dist.astype(f32), dist >= 0)
        o_c = jnp.einsum('bhgqn,bhnd->bhgqd', p_c, vcmp)
        imp = jnp.einsum('bhgqn,nj->bhqj', p_c, ov)
        cur = (t // SB)[:, None]
        j = jnp.arange(n_sel)[None, :]
        forced = (j == 0) | (j == cur) | (j == cur - 1)
        imp = jnp.where(j > cur, -1.0, imp + jnp.where(forced, A_FORCE_BONUS, 0.0))
        _, sel = lax.top_k(imp, topk)
        ksel = ks_blk[bi, hi, sel].reshape(B, Hk, TQ, topk * SB, dh)
        vsel = vs_blk[bi, hi, sel].reshape(B, Hk, TQ, topk * SB, dh)
        pos = (sel[..., None] * SB + jnp.arange(SB)).reshape(B, Hk, TQ, topk * SB)
        dist = (t[:, None] - pos)[:, :, None]
        p_s = masked_softmax(jnp.einsum('bhgqd,bhqkd->bhgqk', qb, ksel) - slopes * dist.astype(f32), dist >= 0)
        o_s = jnp.einsum('bhgqk,bhqkd->bhgqd', p_s, vsel)
        kwin = lax.dynamic_slice_in_dim(kw_p, q0, TQ + W, axis=2)
        vwin = lax.dynamic_slice_in_dim(vw_p, q0, TQ + W, axis=2)
        spos = q0 - W + jnp.arange(TQ + W)
        dist = t[:, None] - spos[None, :]
        mask = (dist >= 0) & (dist < W) & (spos[None, :] >= 0)
        p_w = masked_softmax(jnp.einsum('bhgqd,bhkd->bhgqk', qb, kwin) - slopes * dist.astype(f32), mask)
        o_w = jnp.einsum('bhgqk,bhkd->bhgqd', p_w, vwin)
        return jnp.stack([o_c, o_s, o_w])

    outs = lax.map(block, jnp.arange(S // TQ))
    outs = outs.transpose(1, 2, 0, 5, 3, 4, 6).reshape(3, B, S, H, dh)
    gates = jax.nn.sigmoid(gl.astype(f32)).reshape(B, S, 3, H).transpose(2, 0, 1, 3)[..., None]
    o = jnp.sum(gates * outs, axis=0).reshape(B, S, H * dh)
    return (o.astype(u.dtype) * jax.nn.silu(z)) @ w_out


def diff_mixer(u, w_in, w_out, lam, sub_gain, lambda_init):
    B, S, _ = u.shape
    H, dh, TQ = B_HEADS, HEAD_DIM, Q_BLOCK
    f32 = jnp.float32
    q, k, v, z = split_cols(u @ w_in, (D_MODEL,) * 4)
    q = q.reshape(B, S, H, 2, dh).transpose(0, 2, 3, 1, 4).astype(f32) * dh ** -0.5
    k = k.reshape(B, S, H, 2, dh).transpose(0, 2, 3, 1, 4).astype(f32)
    v = v.reshape(B, S, H, 2 * dh).transpose(0, 2, 1, 3).astype(f32)
    lamf = lam.astype(f32)
    lam_full = jnp.exp(jnp.sum(lamf[0] * lamf[1])) - jnp.exp(jnp.sum(lamf[2] * lamf[3])) + lambda_init
    slopes = alibi_slopes(H).reshape(1, H, 1, 1, 1)
    kpos = jnp.arange(S)

    def block(i):
        q0 = i * TQ
        t = q0 + jnp.arange(TQ)
        qb = lax.dynamic_slice_in_dim(q, q0, TQ, axis=3)
        dist = t[:, None] - kpos[None, :]
        p = masked_softmax(jnp.einsum('bhmqd,bhmkd->bhmqk', qb, k) - slopes * dist.astype(f32), dist >= 0)
        a = p[:, :, 0] - lam_full * p[:, :, 1]
        return jnp.einsum('bhqk,bhkd->bhqd', a, v)

    o = lax.map(block, jnp.arange(S // TQ))
    o = o.transpose(1, 0, 3, 2, 4).reshape(B, S, H, 2 * dh)
    o = (rms_norm(o, sub_gain) * (1.0 - lambda_init)).reshape(B, S, H * 2 * dh)
    return (o.astype(u.dtype) * jax.nn.silu(z)) @ w_out


def dilated_mixer(u, w_in, w_out):
    B, S, _ = u.shape
    G, Hg, dh, dv, TQ = C_GROUPS, C_HEADS, HEAD_DIM, C_V_DIM, C_Q_BLOCK
    f32 = jnp.float32
    qk = G * Hg * dh
    q, k, v, z = split_cols(u @ w_in, (qk, qk, D_MODEL, D_MODEL))
    q = q.reshape(B, S, G, Hg, dh).transpose(2, 0, 3, 1, 4).astype(f32) * dh ** -0.5
    k = k.reshape(B, S, G, Hg, dh).transpose(2, 0, 3, 1, 4).astype(f32)
    v = v.reshape(B, S, Hg, dv).transpose(0, 2, 1, 3).astype(f32)
    slopes = alibi_slopes(G * Hg).reshape(G, Hg)

    def block(i):
        q0 = i * TQ
        t = q0 + jnp.arange(TQ)
        outs, lses = [], []
        for g, (w, d) in enumerate(C_PATTERNS):
            offs = d * np.arange(w // d + 1)
            p = t[:, None] - offs[None, :]
            valid = p >= 0
            pc = jnp.maximum(p, 0)
            kg = jnp.take(k[g], pc, axis=2)
            vg = jnp.take(v, pc, axis=2)
            qb = lax.dynamic_slice_in_dim(q[g], q0, TQ, axis=2)
            s = jnp.einsum('bhqd,bhqnd->bhqn', qb, kg) - slopes[g][None, :, None, None] * offs.astype(np.float32)
            s = jnp.where(valid, s, NEG_INF)
            m = jnp.max(s, axis=-1, keepdims=True)
            e = jnp.where(valid, jnp.exp(s - m), 0.0)
            l = jnp.sum(e, axis=-1, keepdims=True)
            outs.append(jnp.einsum('bhqn,bhqnd->bhqd', e / l, vg))
            lses.append(m[..., 0] + jnp.log(l[..., 0]))
        wts = jax.nn.softmax(jnp.stack(lses), axis=0)[..., None]
        return jnp.sum(wts * jnp.stack(outs), axis=0)

    o = unblock(lax.map(block, jnp.arange(S // TQ)))
    return (o.astype(u.dtype) * jax.nn.silu(z)) @ w_out


def stick_mixer(u, w_in, w_out):
    B, S, _ = u.shape
    H, dh, TQ = D_HEADS, HEAD_DIM, Q_BLOCK
    f32 = jnp.float32
    q, k, v, z = split_cols(u @ w_in, (D_MODEL,) * 4)
    q = q.reshape(B, S, H, dh).transpose(0, 2, 1, 3).astype(f32) * dh ** -0.5
    k = k.reshape(B, S, H, dh).transpose(0, 2, 1, 3).astype(f32)
    v = v.reshape(B, S, H, dh).transpose(0, 2, 1, 3).astype(f32)
    kpos = jnp.arange(S)

    def block(i):
        q0 = i * TQ
        t = q0 + jnp.arange(TQ)
        qb = lax.dynamic_slice_in_dim(q, q0, TQ, axis=2)
        logits = jnp.einsum('bhqd,bhkd->bhqk', qb, k)
        causal = kpos[None, :] < t[:, None]
        log_om = jnp.where(causal, jax.nn.log_sigmoid(-logits), 0.0)
        between = lax.cumsum(log_om, axis=3, reverse=True) - log_om
        a = jnp.where(causal, jnp.exp(jax.nn.log_sigmoid(logits) + between), 0.0)
        return jnp.einsum('bhqk,bhkd->bhqd', a, v)

    o = unblock(lax.map(block, jnp.arange(S // TQ)))
    return (o.astype(u.dtype) * jax.nn.silu(z)) @ w_out


def setup_inputs(seed: int = 0) -> dict:
    key = jax.random.key(seed)
    ks = jax.random.split(key, 20)
    n_a = (DEPTH + 3) // 4
    n_b = (DEPTH + 2) // 4
    n_c = (DEPTH + 1) // 4
    n_d = DEPTH // 4
    L, dh = A_CMP_BLOCK, HEAD_DIM

    def dense(k, shape, fan_in):
        return jax.random.normal(k, shape, jnp.float32) * fan_in ** -0.5

    def gain(k, shape):
        return 1.0 + 0.02 * jax.random.normal(k, shape, jnp.float32)

    return {
        'x': jax.random.normal(ks[0], (BATCH, SEQ, D_MODEL), jnp.float32),
        'norm_pre': gain(ks[1], (DEPTH, D_MODEL)),
        'norm_post': gain(ks[2], (DEPTH, D_MODEL)),
        'a_w_in': dense(ks[3], (n_a, D_MODEL, A_IN), D_MODEL),
        'a_w_out': dense(ks[4], (n_a, D_MODEL, D_MODEL), D_MODEL),
        'a_cmp_pos_k': 0.1 * jax.random.normal(ks[5], (n_a, L, dh), jnp.float32),
        'a_cmp_pos_v': 0.1 * jax.random.normal(ks[6], (n_a, L, dh), jnp.float32),
        'a_cmp_w1_k': dense(ks[7], (n_a, L * dh, A_CMP_HIDDEN), L * dh),
        'a_cmp_w2_k': dense(ks[8], (n_a, A_CMP_HIDDEN, dh), A_CMP_HIDDEN),
        'a_cmp_w1_v': dense(ks[9], (n_a, L * dh, A_CMP_HIDDEN), L * dh),
        'a_cmp_w2_v': dense(ks[10], (n_a, A_CMP_HIDDEN, dh), A_CMP_HIDDEN),
        'b_w_in': dense(ks[11], (n_b, D_MODEL, B_IN), D_MODEL),
        'b_w_out': dense(ks[12], (n_b, D_MODEL, D_MODEL), D_MODEL),
        'b_lambda': 0.1 * jax.random.normal(ks[13], (n_b, 4, dh), jnp.float32),
        'b_sub_gain': gain(ks[14], (n_b, 2 * dh)),
        'c_w_in': dense(ks[15], (n_c, D_MODEL, C_IN), D_MODEL),
        'c_w_out': dense(ks[16], (n_c, D_MODEL, D_MODEL), D_MODEL),
        'd_w_in': dense(ks[17], (n_d, D_MODEL, D_IN), D_MODEL),
        'd_w_out': dense(ks[18], (n_d, D_MODEL, D_MODEL), D_MODEL),
    }


def reference(x, norm_pre, norm_post, a_w_in, a_w_out, a_cmp_pos_k, a_cmp_pos_v, a_cmp_w1_k, a_cmp_w2_k,
              a_cmp_w1_v, a_cmp_w2_v, b_w_in, b_w_out, b_lambda, b_sub_gain, c_w_in, c_w_out, d_w_in, d_w_out):
    h = x
    for i in range(DEPTH):
        m, j = i % N_MIXERS, i // N_MIXERS
        u = rms_norm(h, norm_pre[i])
        if m == 0:
            y = nsa_mixer(u, a_w_in[j], a_w_out[j], a_cmp_pos_k[j], a_cmp_pos_v[j],
                          a_cmp_w1_k[j], a_cmp_w2_k[j], a_cmp_w1_v[j], a_cmp_w2_v[j])
        elif m == 1:
            lambda_init = 0.8 - 0.6 * math.exp(-0.3 * i)
            y = diff_mixer(u, b_w_in[j], b_w_out[j], b_lambda[j], b_sub_gain[j], lambda_init)
        elif m == 2:
            y = dilated_mixer(u, c_w_in[j], c_w_out[j])
        else:
            y = stick_mixer(u, d_w_in[j], d_w_out[j])
        h = h + rms_norm(y, norm_post[i])
    return h
```

```python
import math
from contextlib import ExitStack

import numpy as np
import ml_dtypes
import concourse.bass as bass
import concourse.mybir as mybir
from concourse.bass_utils import run_bass_kernel_spmd

F32 = mybir.dt.float32
BF16 = mybir.dt.bfloat16
AF = mybir.ActivationFunctionType
ALU = mybir.AluOpType
NPBF = ml_dtypes.bfloat16

D = 1024
NEG = -30000.0
EPS = 1e-6


class Sched:
    ENG = ("pe", "act", "dve", "pool", "sp")

    def __init__(self, nc):
        self.nc = nc
        self.eng = dict(pe=nc.tensor, act=nc.scalar, dve=nc.vector, pool=nc.gpsimd, sp=nc.sync)
        self.ops = []
        self.last_w = {}
        self.readers = {}
        self.bar = set()
        self.last_on = {}

    def add(self, eng, fn, kw, reads=(), writes=(), dma=None, inc=16):
        i = len(self.ops)
        bk = [r for r in reads if isinstance(r, tuple) and r[0] == "bank"]
        if bk:
            reads = [r for r in reads if not (isinstance(r, tuple) and r[0] == "bank")]
            writes = list(writes) + bk
        deps = set(self.bar)
        for r in reads:
            w = self.last_w.get(r)
            if w is not None:
                deps.add(w)
        for w in writes:
            lw = self.last_w.get(w)
            if lw is not None:
                deps.add(lw)
            rd = self.readers.get(w)
            if rd:
                deps.update(rd.values())
        self.ops.append([eng, (fn, kw), deps, dma, None, inc])
        for w in writes:
            self.last_w[w] = i
            self.readers[w] = {}
        for r in reads:
            self.readers.setdefault(r, {})[(eng, dma) if dma is not None else eng] = i
        self.last_on[(eng, dma)] = i
        return i

    def barrier(self):
        self.bar = set(v for (e, k), v in self.last_on.items() if not (isinstance(k, tuple) and k and k[0] == "cc"))

    def emit(self):
        nc = self.nc
        ops = self.ops
        needed = [False] * len(ops)
        for op in ops:
            for d in op[2]:
                dop = ops[d]
                if dop[3] is None and op[3] is None and dop[0] == "pe" and op[0] == "pe":
                    continue
                needed[d] = True
        esem = {e: nc.alloc_semaphore(name="sem_" + e) for e in self.ENG}
        dsem = {}
        cnt = {e: 0 for e in self.ENG}
        dcnt = {}
        for i, op in enumerate(ops):
            if op[3] is not None:
                k = op[3]
                if k not in dsem:
                    dsem[k] = nc.alloc_semaphore(name="dma_%d" % len(dsem))
                    dcnt[k] = 0
                dcnt[k] += op[5]
                op[4] = (k, dsem[k], dcnt[k])
            elif needed[i]:
                cnt[op[0]] += 1
                op[4] = (op[0], esem[op[0]], cnt[op[0]])
        seen = {e: {} for e in self.ENG}
        for i, op in enumerate(ops):
            e = op[0]
            E = self.eng[e]
            waits = {}
            for d in op[2]:
                dop = ops[d]
                if dop[3] is None and op[3] is None and dop[0] == "pe" and e == "pe":
                    continue
                key, sem, val = dop[4]
                if key not in waits or waits[key][1] < val:
                    waits[key] = (sem, val)
            for key, (sem, val) in waits.items():
                if seen[e].get(key, 0) >= val:
                    continue
                E.wait_ge(sem, val)
                seen[e][key] = val
            ins = op[1][0](**op[1][1])
            if op[4] is not None:
                ins.then_inc(op[4][1], op[5] if op[3] is not None else 1)
        for k, sem in dsem.items():
            nc.sync.wait_ge(sem, dcnt[k])
        self.n_ops = len(ops)


class Arena:
    def __init__(self, nc):
        self.nc = nc
        base = (nc.sbuf_base + 63) // 64 * 64
        size = nc.sbuf_top - base - 64
        self.slab = nc.alloc_sbuf_tensor("arena", [128, size], mybir.dt.uint8)
        self.base = base
        self.top = base
        self.end = base + size
        self.n = 0

    def mark(self):
        return self.top

    def reset(self, m):
        self.top = m

    def alloc(self, name, shape, dtype):
        nbytes = int(np.prod(shape[1:])) * (4 if dtype == F32 else 2)
        off = (self.top + 63) // 64 * 64
        assert off + nbytes <= self.end, "SBUF overflow at %s: need %d, have %d" % (name, nbytes, self.end - off)
        self.top = off + nbytes
        self.n += 1
        return self.nc.alloc_sbuf_tensor_at("%s_%d" % (name, self.n), list(shape), dtype, offset=off)


def alibi(n):
    return np.array([2.0 ** (-8.0 * (i + 1) / n) for i in range(n)], np.float32)


def band_strip(width, shift, pred):
    k = np.arange(128)[:, None]
    x = np.arange(width)[None, :]
    d = x - shift - k
    return np.where(pred(d), 0.0, NEG).astype(NPBF)


class LayerProg:
    def __init__(self, L, S, final=False, T=None, res=True, fused=False):
        self.L, self.S, self.final = L, S, final
        self.res = res or final
        self.fused = fused
        self.pfx = ""
        self.T = T if final else S
        nc = self.nc = bass.Bass("TRN2", target_bir_lowering=False)
        self.sc = Sched(nc)
        self.ar = Arena(nc)
        self.banks = [nc.alloc_psum_tensor("bank%d" % i, [128, 512], F32) for i in range(8)]
        self.din = {}
        self.dout = {}
        self.build()
        self.sc.emit()

    def mm(self, out, lhsT, rhs, start, stop, reads, writes, skip=False):
        kw = dict(out=out, lhsT=lhsT, rhs=rhs, start=start, stop=stop)
        if skip:
            kw["skip_group_check"] = True
        self.sc.add("pe", self.nc.tensor.matmul, kw, reads, writes)

    def tr(self, out, in_, reads, writes, ident=None):
        if ident is None:
            ident = self.ident[:]
        self.sc.add("pe", self.nc.tensor.transpose, dict(out=out, in_=in_, identity=ident), list(reads) + ["ident"], writes)

    def untranspose(self, otbank, ncols, dst, eng):
        k = self.ot_i % 2
        self.ot_i += 1
        ot = self.OT32[k]
        if eng == "act":
            self.act(ot[0:ncols, :], self.banks[otbank][0:ncols, :], AF.Copy, [("bank", otbank)], [("OT32", k)])
        else:
            self.cp("dve", ot[0:ncols, :], self.banks[otbank][0:ncols, :], [("bank", otbank)], [("OT32", k)])
        for sub in range(4):
            ob, c0 = dst[sub]
            self.tr(self.banks[ob][:, c0:c0 + ncols], ot[0:ncols, sub * 128:(sub + 1) * 128], [("OT32", k)], [("bank", ob)],
                    ident=self.identf[0:ncols, 0:ncols])

    def act(self, out, in_, func, reads, writes, **kw):
        self.sc.add("act", self.nc.scalar.activation, dict(out=out, in_=in_, func=func, **kw), reads, writes)

    def veng(self, eng):
        return self.nc.vector if eng == "dve" else self.nc.gpsimd

    def tt(self, eng, out, in0, in1, op, reads, writes):
        self.sc.add(eng, self.veng(eng).tensor_tensor, dict(out=out, in0=in0, in1=in1, op=op), reads, writes)

    def stt(self, out, in0, scalar, in1, op0, op1, reads, writes):
        self.sc.add("dve", self.nc.vector.scalar_tensor_tensor,
                    dict(out=out, in0=in0, scalar=scalar, in1=in1, op0=op0, op1=op1), reads, writes)

    def ts(self, eng, out, in0, s1, s2, op0, op1, reads, writes):
        kw = dict(out=out, in0=in0, scalar1=s1, scalar2=s2, op0=op0)
        if op1 is not None:
            kw["op1"] = op1
        self.sc.add(eng, self.veng(eng).tensor_scalar, kw, reads, writes)

    def cp(self, eng, out, in_, reads, writes):
        if eng == "act":
            self.sc.add("act", self.nc.scalar.copy, dict(out=out, in_=in_), reads, writes)
        else:
            self.sc.add(eng, self.veng(eng).tensor_copy, dict(out=out, in_=in_), reads, writes)

    def dma(self, eng, out, in_, reads, writes, key):
        E = self.nc.sync if eng == "sp" else self.nc.gpsimd
        self.sc.add(eng, E.dma_start, dict(out=out, in_=in_), reads, writes, dma=key)

    def inp(self, name, shape, dtype):
        name = self.pfx + name
        t = self.nc.dram_tensor(name, list(shape), dtype, kind="ExternalInput").ap()
        self.din[name] = t
        return t

    def outp(self, name, shape, dtype):
        t = self.nc.dram_tensor(name, list(shape), dtype, kind="ExternalOutput").ap()
        self.dout[name] = t
        return t

    def load_const(self, name, shape, dtype):
        src = self.inp(name, shape, dtype)
        dst = self.ar.alloc(name, shape, dtype)
        self.dma("sp", dst[:], src, [], [name], name)
        return dst

    def load_cast(self, name, shape):
        src = self.inp(name, shape, F32)
        dst = self.ar.alloc(name, shape, BF16)
        self.lc_i = getattr(self, "lc_i", 0)
        for c in range(shape[1]):
            i = self.lc_i % 2
            self.lc_i += 1
            st = self.wstage[i]
            keys = self.stage_keys[i]
            self.dma("sp", st[:, 0:shape[2]], src[:, c, :], [], keys, ("wstage", i))
            self.cp("pool", dst[:, c, :], st[:, 0:shape[2]], keys, [(name, c)])
        return dst

    def bank_bf16(self, i):
        return self.banks[i][:].bitcast(BF16)

    def run_stages(self, n, stages):
        for step in range(n + len(stages) - 1):
            for k, fn in enumerate(stages):
                i = step - k
                if 0 <= i < n:
                    fn(i)

    def prologue(self, fm_groups, tm_groups):
        nc, sc, ar, T = self.nc, self.sc, self.ar, self.T
        NT = T // 128
        has_res = self.res
        final = self.final
        banks = self.banks
        fused = self.fused
        h_in = self.h_src if fused else self.inp("h_in", [T, D], F32)
        if not final:
            ub = [ar.alloc("ub", [128, D], BF16) for _ in range(2)]
            uT = [ar.alloc("uT", [128, 8, 512], BF16) for _ in range(2)]
            wstage = [u_[:].rearrange("p c t -> p (c t)").bitcast(F32) for u_ in uT]
            stage_keys = [[("uT", i, q) for q in range(4)] for i in range(2)]
        else:
            ws = [ar.alloc("wstage", [128, 1024], F32) for _ in range(2)]
            wstage = [w_[:] for w_ in ws]
            stage_keys = [[("wstage", i)] for i in range(2)]
        self.wstage, self.stage_keys = wstage, stage_keys
        if has_res:
            if fused:
                ogTs = [a_.rearrange("(c p) t -> p c t", p=128) for a_ in self.allg]
            else:
                ogT = self.inp("ogT", [D, T], BF16).rearrange("(c p) t -> p c t", p=128)
            woutb = self.load_cast("wout", [128, 8, D])
            gpost = self.load_const("gpost", [128, D], F32)
            h_out = self.h_dst if fused else self.outp("h_out", [T, D], F32)
            ogc = [ar.alloc("ogc", [128, 8, 128], BF16) for _ in range(2)]
        if not final:
            gpre = self.load_const("gpre", [128, D], F32)
            winb = self.winb = self.load_cast("win", [128, 8, self.ncols])
        hin = [ar.alloc("hin", [128, D], F32) for _ in range(3)]
        stt_ = [ar.alloc("st", [128, 8], F32) for _ in range(4)]
        YB = [(0, 1), (2, 3)]
        PT = 4
        PJ = [5, 6, 7]
        pj = [0]

        def rstd_ops(src, dst, st, key):
            self.act(st[:, dst:dst + 1], st[:, src:src + 1], AF.Ln, [key], [key], scale=1.0 / D, bias=EPS)
            self.act(st[:, dst:dst + 1], st[:, dst:dst + 1], AF.Exp, [key], [key], scale=-0.5)

        def P0(tt):
            hb = hin[tt % 3]
            hk = ("hin", tt % 3)
            self.dma("sp", hb[:], h_in[tt * 128:(tt + 1) * 128, :], [], [hk], hk)
            if has_res:
                oc = ogc[tt % 2]
                ok = ("ogc", tt % 2)
                if fused:
                    j, r = divmod(tt, 16)
                    self.dma("sp", oc[:], ogTs[j][:, :, r * 128:(r + 1) * 128], [("ogall", j)], [ok], ok)
                else:
                    self.dma("sp", oc[:], ogT[:, :, tt * 128:(tt + 1) * 128], [], [ok], ok)

        def P1(tt):
            if has_res:
                oc = ogc[tt % 2]
                ok = ("ogc", tt % 2)
                yb = YB[tt % 2]
                for half in range(2):
                    for c in range(8):
                        self.mm(banks[yb[half]][:], oc[:, c, :], woutb[:, c, half * 512:(half + 1) * 512],
                                c == 0, c == 7, [ok, ("wout", c)], [("bank", yb[half])])

        def P2a(tt):
            if not has_res:
                return
            hb = hin[tt % 3]
            hk = ("hin", tt % 3)
            st = stt_[tt % 4]
            sk = ("st", tt % 4)
            yb = YB[tt % 2]
            jk = ogc[tt % 2][:].rearrange("p c t -> p (c t)")
            for half in range(2):
                hs = slice(half * 512, (half + 1) * 512)
                self.act(jk[:, hs], banks[yb[half]][:], AF.Square, [("bank", yb[half])], [sk, ("ogc", tt % 2)],
                         accum_out=st[:, half:half + 1])
            self.tt("dve", st[:, 2:3], st[:, 0:1], st[:, 1:2], ALU.add, [sk], [sk])
            rstd_ops(2, 3, st, sk)
            for half in range(2):
                hs = slice(half * 512, (half + 1) * 512)
                yp = banks[yb[half]][:]
                self.stt(yp, yp, st[:, 3:4], gpost[:, hs], ALU.mult, ALU.mult, [("bank", yb[half]), sk, "gpost"], [])
                self.tt("dve", hb[:, hs], yp, hb[:, hs], ALU.add, [("bank", yb[half]), hk], [hk])
            self.dma("pool", h_out[tt * 128:(tt + 1) * 128, :], hb[:], [hk], [], ("hst", tt % 3))

        def P2b(tt):
            if final:
                return
            hb = hin[tt % 3]
            hk = ("hin", tt % 3)
            st = stt_[tt % 4]
            sk = ("st", tt % 4)
            self.act(ub[tt % 2][:], hb[:], AF.Square, [hk], [sk, ("ub", tt % 2)], accum_out=st[:, 4:5])
            rstd_ops(4, 5, st, sk)
            self.stt(ub[tt % 2][:], hb[:], st[:, 5:6], gpre[:], ALU.mult, ALU.mult, [hk, sk, "gpre"], [("ub", tt % 2)])

        def P3(tt):
            if final:
                return
            ch, s = divmod(tt, 4)
            u = ub[tt % 2]
            pt = self.bank_bf16(PT)
            for c in range(8):
                self.tr(pt[:, c * 128:(c + 1) * 128], u[:, c * 128:(c + 1) * 128], [("ub", tt % 2)], [("bank", PT)])
            ut = uT[ch % 2]
            self.cp("dve", ut[:, :, s * 128:(s + 1) * 128], pt[:, 0:1024].rearrange("p (c t) -> p c t", c=8),
                    [("bank", PT)], [("uT", ch % 2, s)])

        def P4(tt):
            if final:
                return
            ch, s = divmod(tt, 4)
            ut = uT[ch % 2]
            for (off, n, evac) in tm_groups:
                bk = PJ[pj[0] % 3]
                pj[0] += 1
                for c in range(8):
                    self.mm(banks[bk][:, 0:n], ut[:, c, s * 128:(s + 1) * 128], winb[:, c, off:off + n], c == 0, c == 7,
                            [("uT", ch % 2, s), ("win", c)], [("bank", bk)])
                evac(banks[bk][:, 0:n], ("bank", bk), tt)
            if s == 3:
                for (off, n, evac) in fm_groups:
                    bk = PJ[pj[0] % 3]
                    pj[0] += 1
                    for c in range(8):
                        self.mm(banks[bk][0:n, :], winb[:, c, off:off + n], ut[:, c, :], c == 0, c == 7,
                                [("uT", ch % 2, q) for q in range(4)] + [("win", c)], [("bank", bk)])
                    evac(banks[bk][0:n, :], ("bank", bk), ch)

        stages = [P1, P2a, P2b, P3, P4]
        for step in range(NT + len(stages) + 1):
            for k, fn in enumerate(stages):
                i = step - 1 - k
                if 0 <= i < NT:
                    fn(i)
            if step < NT:
                P0(step)

    def build(self):
        self.ident = self.load_const("ident", [128, 128], BF16)
        self.identf = self.load_const("identf", [128, 128], F32)
        self.ot_i = 0
        if self.fused:
            return self.build_fused()
        if self.final:
            self.prologue([], [])
            return
        getattr(self, "build_L%d" % self.L)()

    def build_fused(self):
        nc, sc, ar, S = self.nc, self.sc, self.ar, self.S
        self.xin = self.inp("x", [S, D], F32)
        self.hbuf = nc.dram_tensor("hbuf", [S, D], F32).ap()
        NJ = S // 2048
        self.loc = [nc.dram_tensor("ogloc%d" % j, [256, 2048], BF16).ap() for j in range(NJ)]
        self.allg = [nc.dram_tensor("ogall%d" % j, [1024, 2048], BF16).ap() for j in range(NJ)]
        base = ar.mark()
        for L in range(4):
            self.L, self.pfx, self.res, self.final = L, "L%d_" % L, L > 0, False
            self.h_src = self.xin if L <= 1 else self.hbuf
            self.h_dst = self.hbuf
            getattr(self, "build_L%d" % L)()
            sc.barrier()
            ar.reset(base)
        self.L, self.pfx, self.res, self.final, self.T = 4, "F_", True, True, S
        self.h_src = self.hbuf
        self.h_dst = self.outp("out", [S, D], F32)
        self.prologue([], [])

    def fm_evac(self, dst, pr, scale, p0=0, p1=128):
        def evac(ps, pk, ch):
            self.act(dst[p0:p1, pr, ch * 512:(ch + 1) * 512], ps[p0:p1, :], AF.Copy, [pk], [(id(dst), pr, ch)], scale=scale)
        return evac

    def batch_silu(self, ZS, NT):
        for t0 in range(0, NT, 4):
            t1 = min(NT, t0 + 4)
            ks = [("ZS", t) for t in range(t0, t1)]
            self.act(ZS[:, t0:t1, :], ZS[:, t0:t1, :], AF.Silu, ks, ks)

    def store_og(self, ZS, NT):
        if self.fused:
            return
        og = self.outp("og", [self.S, 256], BF16)
        ogv = og.rearrange("(t p) f -> p t f", p=128)
        for t0 in range(0, NT, 16):
            t1 = min(NT, t0 + 16)
            self.dma("pool", ogv[:, t0:t1, :], ZS[:, t0:t1, :], [("ZS", t) for t in range(t0, t1)], [], ("ogst", t0))

    def ogT_begin(self):
        if self.fused:
            self.ogstg = [self.ar.alloc("ogstg", [128, 2, 512], BF16) for _ in range(2)]

    def ogT_qb(self, ZS, qb):
        if not self.fused:
            return
        nc, sc = self.nc, self.sc
        TBK = 7
        tb = self.bank_bf16(TBK)
        st = self.ogstg[qb % 2]
        for s4 in range(4):
            tt = 4 * qb + s4
            for fh in range(2):
                self.tr(tb[:, fh * 128:(fh + 1) * 128], ZS[:, tt, fh * 128:(fh + 1) * 128], [("ZS", tt)], [("bank", TBK)])
            self.cp("dve", st[:, :, s4 * 128:(s4 + 1) * 128], tb[:, 0:256].rearrange("p (h t) -> p h t", h=2),
                    [("bank", TBK)], [("ogstg", qb % 2, s4)])
        j, k = divmod(qb, 4)
        dst = self.loc[j].rearrange("(h p) t -> p h t", p=128)[:, :, k * 512:(k + 1) * 512]
        self.dma("pool", dst, st[:], [("ogstg", qb % 2, q) for q in range(4)], [("ogloc", j, k)], ("ogst", qb % 2))
        if k == 3:
            sc.add("pool", nc.gpsimd.collective_compute,
                   dict(kind="AllGather", op=ALU.bypass, replica_groups=[[0, 1, 2, 3], [4, 5, 6, 7]],
                        ins=[self.loc[j].opt()], outs=[self.allg[j].opt()]),
                   [("ogloc", j, q) for q in range(4)], [("ogall", j)], dma=("cc", j), inc=1)

    def build_L3(self):
        nc, sc, ar, S = self.nc, self.sc, self.ar, self.S
        NT = S // 128
        NQB = S // 512
        banks = self.banks
        self.ncols = 1024
        QT = ar.alloc("QT", [128, 2, S], BF16)
        KT = ar.alloc("KT", [128, 2, S], BF16)
        V = ar.alloc("V", [128, NT, 256], BF16)
        ZS = ar.alloc("ZS", [128, NT, 256], BF16)
        cms = self.load_const("cms", [128, 896], BF16)
        tri = self.load_const("tri", [128, 128], BF16)
        onesn = self.load_const("onesn", [128, 128], BF16)

        def tm_evac(ps, pk, tt):
            self.cp("dve", V[:, tt, :], ps[:, 0:256], [pk], [("V", tt)])
            self.act(ZS[:, tt, :], ps[:, 256:512], AF.Copy, [pk], [("ZS", tt)])

        fmg = [(0, 128, self.fm_evac(QT, 0, 0.125)), (128, 128, self.fm_evac(QT, 1, 0.125)),
               (256, 128, self.fm_evac(KT, 0, 1.0)), (384, 128, self.fm_evac(KT, 1, 1.0))]
        mark = ar.mark()
        self.prologue(fmg, [(512, 512, tm_evac)])
        sc.barrier()
        ar.reset(mark)
        self.alloc_ot()
        self.ogT_begin()
        self.batch_silu(ZS, NT)
        E32p = [ar.alloc("E32", [128, 2, 512], F32) for _ in range(3)]
        SPp = [ar.alloc("SP", [128, 2, 512], BF16) for _ in range(2)]
        E32 = [[E32p[k][:, h, :] for k in range(3)] for h in range(2)]
        SP = [[SPp[k][:, h, :] for k in range(2)] for h in range(2)]
        self.l3_pairbufs = (E32p, SPp)
        MACC = [[ar.alloc("MACC", [128, 512], BF16) for _ in range(2)] for _ in range(2)]
        EC = [[ar.alloc("EC", [128, 512], F32) for _ in range(2)] for _ in range(2)]
        AW = [[ar.alloc("AW", [128, 512], BF16) for _ in range(2)] for _ in range(2)]
        for p in range(2):
            for qb in range(NQB):
                self.l3_pair(p, qb, QT, KT, V, ZS, cms, tri, onesn, E32, SP, MACC, EC, AW)
                if p == 1:
                    self.ogT_qb(ZS, qb)
        self.store_og(ZS, NT)

    def build_L0(self):
        nc, sc, ar, S = self.nc, self.sc, self.ar, self.S
        NT, NQB, NSEL = S // 128, S // 512, S // 64
        NC = S // 16
        NCT = max(1, NC // 128)
        NCP = NCT * 128
        NCMP = NC - 1
        dvc = 65 + NSEL
        banks = self.banks
        self.ncols = 1164
        QT = ar.alloc("QT", [128, 2, S], BF16)
        KS2 = ar.alloc("KS2", [128, 1, S], BF16)
        KW2 = ar.alloc("KW2", [128, 1, S], BF16)
        VS = ar.alloc("VS", [128, NT, 65], BF16)
        VW = ar.alloc("VW", [128, NT, 65], BF16)
        ZS = ar.alloc("ZS", [128, NT, 256], BF16)
        G = ar.alloc("G", [128, NT, 12], F32)
        KCM2 = ar.alloc("KCM2", [128, 1, NCP], BF16)
        VEXT = ar.alloc("VEXT", [128, NCT, dvc], BF16)
        sc.add("pool", nc.gpsimd.memset, dict(ap=VS[:, :, 64:65], constant=1.0), [], ["Vones"])
        sc.add("pool", nc.gpsimd.memset, dict(ap=VW[:, :, 64:65], constant=1.0), ["Vones"], ["Vones"])
        ovaug = self.inp("ovaug", [128, NCT, NSEL + 1], BF16)
        self.dma("sp", VEXT[:, :, 64:dvc], ovaug, [], ["ovaug"], "ovaug")
        mark1 = ar.mark()
        KC2 = ar.alloc("KC2", [128, S + 16], BF16)
        VC2 = ar.alloc("VC2", [128, S + 16], BF16)
        mark2 = ar.mark()

        def shift_evac(dst, name):
            def evac(ps, pk, ch):
                c0 = ch * 512
                self.act(dst[0:64, c0:c0 + 512], ps[0:64, :], AF.Copy, [pk], [(name, ch, 0)])
                if ch == 0:
                    self.act(dst[64:128, 0:511], ps[64:128, 1:512], AF.Copy, [pk], [(name, ch, 1)])
                else:
                    self.act(dst[64:128, c0 - 1:c0 + 511], ps[64:128, :], AF.Copy, [pk], [(name, ch, 1)])
            return evac

        def tm_evac(ps, pk, tt):
            self.cp("dve", VS[:, tt, 0:64], ps[:, 0:64], [pk, "Vones"], [(id(VS), tt)])
            self.cp("dve", VW[:, tt, 0:64], ps[:, 64:128], [pk, "Vones"], [(id(VW), tt)])
            self.act(G[:, tt, :], ps[:, 128:140], AF.Copy, [pk], [("G", tt)])
            self.act(ZS[:, tt, :], ps[:, 140:396], AF.Copy, [pk], [("ZS", tt)])

        fmg = [(0, 128, self.fm_evac(QT, 0, 0.125)), (128, 128, self.fm_evac(QT, 1, 0.125)),
               (256, 128, shift_evac(KC2, "KC2")), (384, 128, shift_evac(VC2, "VC2")),
               (512, 128, self.fm_evac(KS2, 0, 1.0)), (640, 128, self.fm_evac(KW2, 0, 1.0))]
        self.prologue(fmg, [(768, 396, tm_evac)])
        sc.barrier()
        ar.reset(mark2)
        wst = [ar.alloc("cwst", [128, 256], F32) for _ in range(2)]
        self.wstage = [w_[:] for w_ in wst]
        self.stage_keys = [[("cwst", i)] for i in range(2)]
        w1 = {kv: self.load_cast("w1" + kv, [128, 16, 256]) for kv in "kv"}
        w2k = self.load_cast("w2k", [128, 2, 128])
        w2v = self.load_cast("w2v", [128, 2, 64])
        pos2 = {kv: self.load_cast("pos2" + kv, [128, 1, 16]) for kv in "kv"}
        posb = ar.alloc("posb", [128, 4], F32)
        HID = {kv: ar.alloc("HID" + kv, [128, 2, NCP], BF16) for kv in "kv"}
        for kv in "kv":
            sc.add("pool", nc.gpsimd.memset, dict(ap=HID[kv][:], constant=0.0), [], [("HID" + kv, 0), ("HID" + kv, 1)])
        X32 = [ar.alloc("X32", [128, 512], F32) for _ in range(2)]
        X2 = [ar.alloc("X2", [128, 512], F32) for _ in range(2)]
        bkp = 6
        for qi, (kv, half) in enumerate([("k", 0), ("k", 1), ("v", 0), ("v", 1)]):
            for c in range(16):
                self.mm(banks[bkp][:, qi:qi + 1], w1[kv][:, c, half * 128:(half + 1) * 128], pos2[kv][:, 0, c:c + 1],
                        c == 0, c == 15, [("w1" + kv, c), ("pos2" + kv, 0)], [("bank", bkp)], skip=True)
        self.cp("dve", posb[:], banks[bkp][:, 0:4], [("bank", bkp)], ["posb"])
        src = {"k": KC2, "v": VC2}
        it = 0
        for kv in "kv":
            for half in range(2):
                bk = (0, 1)[it % 2]
                for c in range(16):
                    self.mm(banks[bk][:, 0:NCMP], w1[kv][:, c, half * 128:(half + 1) * 128],
                            src[kv][:, 2 * c:2 * c + 16 * (NCMP - 1) + 1:16], c == 0, c == 15, [("w1" + kv, c)], [("bank", bk)])
                x, x2 = X32[it % 2][:, 0:NCMP], X2[it % 2][:, 0:NCMP]
                xk, x2k = ("X32", it % 2), ("X2", it % 2)
                pcol = {"k": 0, "v": 2}[kv] + half
                self.ts("dve", x, banks[bk][:, 0:NCMP], posb[:, pcol:pcol + 1], None, ALU.add, None, [("bank", bk), "posb"], [xk])
                self.tt("pool", x2, x, x, ALU.mult, [xk], [x2k])
                self.ts("pool", x2, x2, 0.044715, 1.0, ALU.mult, ALU.add, [x2k], [x2k])
                self.tt("pool", x2, x2, x, ALU.mult, [xk, x2k], [x2k])
                self.act(x2, x2, AF.Sigmoid, [x2k], [x2k], scale=1.5957691216057308)
                self.tt("dve", HID[kv][:, half, 0:NCMP], x, x2, ALU.mult, [xk, x2k], [("HID" + kv, half)])
                it += 1
        for half in range(2):
            self.mm(banks[2][:, 0:NCP], w2k[:, half, :], HID["k"][:, half, :], half == 0, half == 1,
                    [("w2k", half), ("HIDk", half)], [("bank", 2)])
        self.cp("dve", KCM2[:, 0, :], banks[2][:, 0:NCP], [("bank", 2)], ["KCM2"])
        for j in range(NCT):
            bk = 3 + j % 2
            for half in range(2):
                self.mm(banks[bk][:, 0:64], HID["v"][:, half, j * 128:(j + 1) * 128], w2v[:, half, :], half == 0, half == 1,
                        [("HIDv", half), ("w2v", half)], [("bank", bk)])
            self.cp("dve", VEXT[:, j, 0:64], banks[bk][:, 0:64], [("bank", bk), "ovaug"], [("VEXT", j)])
        sc.barrier()
        ar.reset(mark1)
        self.alloc_ot()
        self.ogT_begin()
        self.batch_silu(ZS, NT)
        self.act(G[:].rearrange("p t g -> p (t g)"), G[:].rearrange("p t g -> p (t g)"), AF.Sigmoid,
                 [("G", t) for t in range(NT)], [("G", t) for t in range(NT)])
        ND = NT + 3
        cm = self.load_const("cm", [128, 896], BF16)
        wm = self.load_const("wm", [128, 1408], BF16)
        cmk = self.load_const("cmk", [128, 2560], BF16)
        EM = self.load_const("EM", [128, NT, 128], BF16)
        abaseA = self.load_const("abaseA", [128, 4, 512], F32)
        abaseC = self.load_const("abaseC", [128, 4, 512], F32)
        acstA = self.load_const("acstA", [128, 4, ND], F32)
        acstC = self.load_const("acstC", [128, 4, 16], F32)
        adj = self.load_const("adj", [128, 2 * NSEL], F32)
        T32 = [[ar.alloc("T32", [128, 512], F32) for _ in range(2)] for _ in range(3)]
        PW = [[ar.alloc("PW", [128, 512], BF16) for _ in range(2)] for _ in range(3)]
        T32c = [T32[0][0], T32[1][0]]
        PWc = [PW[0][0], PW[1][0], PW[2][0]]
        SB4 = ((0, 1), (2, 3), (4, 5))
        SELT = [ar.alloc("SELT", [128, 512], BF16) for _ in range(2)]
        for i in range(2):
            sc.add("pool", nc.gpsimd.memset, dict(ap=SELT[i][:], constant=0.0), [], [("SELT", i, q) for q in range(4)])
        ACC = [ar.alloc("ACC", [128, 4, 256], F32) for _ in range(2)]
        IMP = ar.alloc("IMP", [128, 4, NSEL], F32)
        M8 = [ar.alloc("M8", [128, 16], F32) for _ in range(2)]
        WK = [ar.alloc("WK", [128, NSEL], F32) for _ in range(2)]
        SR = [ar.alloc("SR", [128, NSEL], BF16) for _ in range(2)]
        fs = [ar.alloc("fs", [128, 12], F32) for _ in range(4)]
        SB = (0, 1)
        pairs = [(2, 3), (4, 5)]
        singles = [2, 3, 4, 5]
        TB = 6
        st8 = {"fi": 0, "si": 0}

        def fin_coef4(lsrc, qb_, gcol):
            f = fs[st8["fi"] % 4]
            fk = ("fs", st8["fi"] % 4)
            st8["fi"] += 1
            for sub, (ap_, bk_) in enumerate(lsrc):
                self.ts("dve", f[:, sub:sub + 1], ap_, 1e-30, None, ALU.max, None, [("bank", bk_)], [fk])
            sc.add("dve", nc.vector.reciprocal, dict(out=f[:, 4:8], in_=f[:, 0:4]), [fk], [fk])
            self.tt("dve", f[:, 8:12], f[:, 4:8], G[:, 4 * qb_:4 * qb_ + 4, gcol], ALU.mult,
                    [fk] + [("G", 4 * qb_ + q) for q in range(4)], [fk])
            return f, fk

        for qb in range(NQB):
            acc = ACC[qb % 2]
            selt = SELT[qb % 2]
            for g in range(4):
                pb = pairs[g % 2]
                oacc = [(pb[0], 0), (pb[0], dvc), (pb[1], 0), (pb[1], dvc)]
                kbs = [j for j in range(NCT) if 2048 * j + 31 <= 512 * qb + 511]
                self.softmax_block(QT, KCM2, 0, 64 * (g % 2), qb, kbs, cmk, 0, abaseC[:, g, :], acstC[:, g, :], 0, VEXT,
                                   lambda j: VEXT[:, j, :], dvc, oacc, (0, 2), SB, T32c, PWc, first=True,
                                   maskfn=lambda delta: delta if delta <= 2048 else None, qt_pr=g // 2, mask_key="cmk",
                                   delta_of=lambda j: 512 * qb - 2048 * j, ci_of=lambda delta: delta // 512,
                                   kkey_of=lambda j: "KCM2", vkey_of=lambda j: ("VEXT", j), acst_key="acstC", abase_key="abaseC")
                f, fk = fin_coef4([(banks[ob_][:, c0_ + dvc - 1:c0_ + dvc], ob_) for (ob_, c0_) in oacc], qb, g)
                for sub in range(4):
                    tq = 4 * qb + sub
                    ob, c0 = oacc[sub]
                    imp_ps = banks[ob][:, c0 + 64:c0 + 64 + NSEL]
                    if g == 0:
                        self.ts("dve", IMP[:, sub, :], imp_ps, f[:, 4 + sub:5 + sub], None, ALU.mult, None, [("bank", ob), fk], [("IMP", sub)])
                    else:
                        self.stt(IMP[:, sub, :], imp_ps, f[:, 4 + sub:5 + sub], IMP[:, sub, :], ALU.mult, ALU.add,
                                 [("bank", ob), fk, ("IMP", sub)], [("IMP", sub)])
                    self.ts("dve", acc[:, sub, g * 64:(g + 1) * 64], banks[ob][:, c0:c0 + 64], f[:, 8 + sub:9 + sub], None, ALU.mult, None,
                            [("bank", ob), fk], [("ACC", qb % 2, sub)])
            for sub in range(4):
                tq = 4 * qb + sub
                imp = IMP[:, sub, :]
                ik = ("IMP", sub)
                a0 = NSEL - 1 - 2 * tq
                self.tt("dve", imp, imp, adj[:, a0:a0 + NSEL], ALU.add, [ik, "adj"], [ik])
                if tq >= 1:
                    self.ts("dve", IMP[:, sub, 0:1], IMP[:, sub, 0:1], 1000.0, None, ALU.add, None, [ik], [ik])
                m8, wk, sr = M8[sub % 2], WK[sub % 2], SR[sub % 2]
                mk, wkk, srk = ("M8", sub % 2), ("WK", sub % 2), ("SR", sub % 2)
                sc.add("dve", nc.vector.max, dict(out=m8[:, 0:8], in_=imp), [ik], [mk])
                sc.add("dve", nc.vector.match_replace, dict(out=wk[:], in_to_replace=m8[:, 0:8], in_values=imp, imm_value=-1e9),
                       [ik, mk], [wkk])
                sc.add("dve", nc.vector.max, dict(out=m8[:, 8:16], in_=wk[:]), [wkk, mk], [mk])
                self.ts("dve", sr[:], imp, m8[:, 15:16], 1.0, ALU.is_ge, ALU.subtract, [ik, mk], [srk])
                tb = self.bank_bf16(TB)
                self.tr(tb[0:NSEL, sub * 128:(sub + 1) * 128], sr[:], [srk], [("bank", TB)])
                self.cp("dve", selt[0:NSEL, sub * 128:(sub + 1) * 128], tb[0:NSEL, sub * 128:(sub + 1) * 128],
                        [("bank", TB)], [("SELT", qb % 2, sub)])
            for br in (2, 1):
                for gp in range(2):
                    otbs = (6, 7)
                    heads = []
                    for h in range(2):
                        g = 2 * gp + h
                        Vb = VW if br == 2 else VS
                        heads.append(dict(bp=64 * h, abase=abaseA[:, g, :], acst=acstA[:, g, :], akeys=("acstA", "abaseA"),
                                          pvT=[(lambda kb, Vb=Vb: Vb[:, kb, :], 65, otbs[h], 0)],
                                          oacc=[(otbs[h], 65 * k) for k in range(4)]))
                    if br == 2:
                        kbs = list(range(4 * qb + 3, max(0, 4 * qb - 4) - 1, -1))
                        self.softmax_pair(QT, KW2, 0, gp, qb, kbs, wm, 384, heads, VW, SB4, T32, PW,
                                          maskfn=lambda delta: delta + 384, mask_key="wm")
                    else:
                        kbs = list(range(4 * qb + 3, -1, -1))
                        self.softmax_pair(QT, KS2, 0, gp, qb, kbs, cm, 384, heads, VS, SB4, T32, PW,
                                          extra=lambda kb: (EM[:, kb, :], selt[:], ["EM"] + [("SELT", qb % 2, q) for q in range(4)]))
                    for h in range(2):
                        g = 2 * gp + h
                        ob = otbs[h]
                        f, fk = fin_coef4([(banks[ob][:, 65 * q + 64:65 * q + 65], ob) for q in range(4)], qb, br * 4 + g)
                        for sub in range(4):
                            a_ = acc[:, sub, g * 64:(g + 1) * 64]
                            self.stt(a_, banks[ob][:, 65 * sub:65 * sub + 64], f[:, 8 + sub:9 + sub], a_, ALU.mult, ALU.add,
                                     [("bank", ob), fk, ("ACC", qb % 2, sub)], [("ACC", qb % 2, sub)])
            for sub in range(4):
                tq = 4 * qb + sub
                self.tt("dve", ZS[:, tq, :], acc[:, sub, :], ZS[:, tq, :], ALU.mult, [("ACC", qb % 2, sub), ("ZS", tq)], [("ZS", tq)])
            self.ogT_qb(ZS, qb)
        self.store_og(ZS, NT)


    def build_L1(self):
        nc, sc, ar, S = self.nc, self.sc, self.ar, self.S
        NT = S // 128
        NQB = S // 512
        ND = NT + 3
        banks, X = self.banks, mybir.AxisListType.X
        self.ncols = 1024
        QT = ar.alloc("QT", [128, 2, S], BF16)
        KT = ar.alloc("KT", [128, 2, S], BF16)
        V = ar.alloc("V", [128, NT, 258], BF16)
        ZS = ar.alloc("ZS", [128, NT, 256], BF16)
        self.sc.add("pool", nc.gpsimd.memset, dict(ap=V[:, :, 128:129], constant=1.0), [], ["Vones"])
        self.sc.add("pool", nc.gpsimd.memset, dict(ap=V[:, :, 257:258], constant=1.0), ["Vones"], ["Vones"])

        def tm_evac(ps, pk, tt):
            self.cp("dve", V[:, tt, 0:128], ps[:, 0:128], [pk, "Vones"], [(id(V), tt)])
            self.cp("dve", V[:, tt, 129:257], ps[:, 128:256], [pk, "Vones", (id(V), tt)], [(id(V), tt)])
            self.act(ZS[:, tt, :], ps[:, 256:512], AF.Copy, [pk], [("ZS", tt)])

        fmg = [(0, 128, self.fm_evac(QT, 0, 0.125)), (128, 128, self.fm_evac(QT, 1, 0.125)),
               (256, 128, self.fm_evac(KT, 0, 1.0)), (384, 128, self.fm_evac(KT, 1, 1.0))]
        mark = ar.mark()
        self.prologue(fmg, [(512, 512, tm_evac)])
        sc.barrier()
        ar.reset(mark)
        self.alloc_ot()
        self.ogT_begin()
        self.batch_silu(ZS, NT)
        cm = self.load_const("cm", [128, 896], BF16)
        abase = self.load_const("abase", [128, 2, 512], F32)
        acst = self.load_const("acst", [128, 2, ND], F32)
        lamr = self.load_const("lamr", [128, 4, 64], F32)
        sgbc = self.load_const("sgbc", [128, 128], F32)
        fs = [ar.alloc("fs", [128, 16], F32) for _ in range(2)]
        RES = [ar.alloc("RES", [128, 4, 129], F32) for _ in range(2)]
        a32 = [ar.alloc("a32", [128, 128], F32) for _ in range(2)]
        o32 = [ar.alloc("o32", [128, 128], F32) for _ in range(2)]
        jk = ar.alloc("jk", [128, 128], F32)
        lam = ar.alloc("lam", [128, 8], F32)
        li = 0.8 - 0.6 * math.exp(-0.3 * self.L)
        pr_ = ar.alloc("lprod", [128, 2, 64], F32)
        self.tt("dve", pr_[:, 0, :], lamr[:, 0, :], lamr[:, 1, :], ALU.mult, ["lamr"], ["lprod"])
        self.tt("dve", pr_[:, 1, :], lamr[:, 2, :], lamr[:, 3, :], ALU.mult, ["lamr", "lprod"], ["lprod"])
        sc.add("dve", nc.vector.tensor_reduce, dict(out=lam[:, 0:2], in_=pr_[:], axis=X, op=ALU.add), ["lprod"], ["lam"])
        self.act(lam[:, 2:4], lam[:, 0:2], AF.Exp, ["lam"], ["lam"])
        self.tt("dve", lam[:, 4:5], lam[:, 3:4], lam[:, 2:3], ALU.subtract, ["lam"], ["lam"])
        self.ts("dve", lam[:, 4:5], lam[:, 4:5], -li, None, ALU.add, None, ["lam"], ["lam"])
        SB4 = ((0, 1), (2, 3))
        OB = ((4, 5), (6, 7))
        T32 = [[ar.alloc("T32", [128, 512], F32) for _ in range(2)] for _ in range(3)]
        PW = [[ar.alloc("PW", [128, 512], BF16) for _ in range(2)] for _ in range(3)]
        fi = 0
        for hl in range(2):
            for qb in range(NQB):
                heads = []
                for m in range(2):
                    heads.append(dict(bp=64 * m, abase=abase[:, hl, :], acst=acst[:, hl, :], akeys=("acst", "abase"),
                                      pvT=[(lambda kb, hl=hl: V[:, kb, hl * 129:hl * 129 + 128], 128, OB[m][0], 0),
                                           (lambda kb, hl=hl: V[:, kb, hl * 129 + 128:hl * 129 + 129], 1, OB[m][1], 128)],
                                      oacc=[(OB[m][0], 0), (OB[m][0], 129), (OB[m][1], 0), (OB[m][1], 129)]))
                self.softmax_pair(QT, KT, hl, hl, qb, list(range(4 * qb + 3, -1, -1)), cm, 384, heads, V, SB4, T32, PW)
                for m in range(2):
                    for half in range(2):
                        bk_ = OB[m][half]
                        self.act(RES[m][:, 2 * half:2 * half + 2, :].rearrange("p s c -> p (s c)"), banks[bk_][:, 0:258], AF.Copy,
                                 [("bank", bk_)], [("RES", m, half)])
                for sub in range(4):
                    tq = qb * 4 + sub
                    f = fs[fi % 2]
                    fk = ("fs", fi % 2)
                    a_, o_ = a32[fi % 2], o32[fi % 2]
                    fi += 1
                    r0, r1 = RES[0][:, sub, :], RES[1][:, sub, :]
                    k0, k1 = ("RES", 0, sub // 2), ("RES", 1, sub // 2)
                    self.cp("dve", f[:, 0:1], r0[:, 128:129], [k0], [fk])
                    self.cp("dve", f[:, 1:2], r1[:, 128:129], [k1, fk], [fk])
                    sc.add("dve", nc.vector.reciprocal, dict(out=f[:, 2:4], in_=f[:, 0:2]), [fk], [fk])
                    self.tt("dve", f[:, 4:5], f[:, 3:4], lam[:, 4:5], ALU.mult, [fk, "lam"], [fk])
                    self.ts("dve", a_[:], r0[:, 0:128], f[:, 2:3], None, ALU.mult, None, [k0, fk], [("a32", fi % 2)])
                    self.stt(o_[:], r1[:, 0:128], f[:, 4:5], a_[:], ALU.mult, ALU.add, [k1, fk, ("a32", fi % 2)], [("o32", fi % 2)])
                    self.act(jk[:], o_[:], AF.Square, [("o32", fi % 2)], [fk], accum_out=f[:, 5:6])
                    self.act(f[:, 6:7], f[:, 5:6], AF.Ln, [fk], [fk], scale=1.0 / 128, bias=EPS)
                    self.act(f[:, 6:7], f[:, 6:7], AF.Exp, [fk], [fk], scale=-0.5, bias=math.log(1.0 - li))
                    self.stt(o_[:], o_[:], f[:, 6:7], sgbc[:], ALU.mult, ALU.mult, [fk, "sgbc", ("o32", fi % 2)], [("o32", fi % 2)])
                    zs = ZS[:, tq, hl * 128:(hl + 1) * 128]
                    self.tt("dve", zs, o_[:], zs, ALU.mult, [("o32", fi % 2), ("ZS", tq)], [("ZS", tq)])
                if hl == 1:
                    self.ogT_qb(ZS, qb)
        self.store_og(ZS, NT)

    def build_L2(self):
        nc, sc, ar, S = self.nc, self.sc, self.ar, self.S
        NT = S // 128
        NQB = S // 512
        banks = self.banks
        self.ncols = 896
        PAT = ((128, 1), (512, 4), (2048, 16))
        QT = ar.alloc("QT", [128, 2, S], BF16)
        KT = ar.alloc("KT", [128, 2, S], BF16)
        V = ar.alloc("V", [128, NT, 257], BF16)
        ZS = ar.alloc("ZS", [128, NT, 256], BF16)
        self.sc.add("pool", nc.gpsimd.memset, dict(ap=V[:, :, 256:257], constant=1.0), [], ["Vones"])

        def tm_evac(ps, pk, tt):
            self.cp("dve", V[:, tt, 0:256], ps[:, 0:256], [pk, "Vones"], [(id(V), tt)])
            self.act(ZS[:, tt, :], ps[:, 256:512], AF.Copy, [pk], [("ZS", tt)])

        fmg = [(0, 128, self.fm_evac(QT, 0, 0.125)), (128, 64, self.fm_evac(QT, 1, 0.125, 0, 64)),
               (192, 128, self.fm_evac(KT, 0, 1.0)), (320, 64, self.fm_evac(KT, 1, 1.0, 0, 64))]
        mark = ar.mark()
        self.prologue(fmg, [(384, 512, tm_evac)])
        sc.barrier()
        ar.reset(mark)
        self.ogT_begin()
        self.batch_silu(ZS, NT)
        ND = 20
        strips = [self.load_const("dm%d" % g, [128, 384 + w + 512], BF16) for g, (w, d) in enumerate(PAT)]
        abase = self.load_const("abase", [128, 3, 512], F32)
        acst = self.load_const("acst", [128, 3, ND], F32)
        T32 = [ar.alloc("T32", [128, 512], F32) for _ in range(2)]
        PW = [ar.alloc("PW", [128, 512], BF16) for _ in range(3)]
        fs = [ar.alloc("fs", [128, 4], F32) for _ in range(2)]
        SB = (0, 1)
        oacc = [(2, 0), (3, 0), (4, 0), (5, 0)]
        fi = 0
        for qb in range(NQB):
            for g, (w, d) in enumerate(PAT):
                pr, bp = ((0, 0), (0, 64), (1, 0))[g]
                kb_min = max(0, (512 * qb - w) // 128)
                kbs = list(range(4 * qb + 3, kb_min - 1, -1))
                self.softmax_block(QT, KT, pr, bp, qb, kbs, strips[g], 384, abase[:, g, :], acst[:, g, :], 3, V,
                                   lambda kb: V[:, kb, :], 257, oacc, (0, 1, 2, 3), SB, T32, PW, first=(g == 0),
                                   maskfn=lambda delta: delta + 384, mask_key="dm%d" % g)
            for sub in range(4):
                tq = qb * 4 + sub
                f = fs[fi % 2]
                fk = ("fs", fi % 2)
                fi += 1
                ob = oacc[sub][0]
                sc.add("dve", nc.vector.reciprocal, dict(out=f[:, 0:1], in_=banks[ob][:, 256:257]), [("bank", ob)], [fk])
                self.stt(ZS[:, tq, :], banks[ob][:, 0:256], f[:, 0:1], ZS[:, tq, :], ALU.mult, ALU.mult,
                         [("bank", ob), fk, ("ZS", tq)], [("ZS", tq)])
            self.ogT_qb(ZS, qb)
        self.store_og(ZS, NT)


    def alloc_ot(self):
        self.OT32 = [self.ar.alloc("OT32", [128, 512], F32) for _ in range(2)]

    def untranspose_multi(self, chunks, eng):
        cps = []
        for (otbank, ncols, dst) in chunks:
            k = self.ot_i % len(self.OT32)
            self.ot_i += 1
            ot = self.OT32[k]
            if eng == "act":
                self.act(ot[0:ncols, :], self.banks[otbank][0:ncols, :], AF.Copy, [("bank", otbank)], [("OT32", k)])
            else:
                self.cp("dve", ot[0:ncols, :], self.banks[otbank][0:ncols, :], [("bank", otbank)], [("OT32", k)])
            cps.append((ot, k, ncols, dst))
        for (ot, k, ncols, dst) in cps:
            for sub in range(4):
                ob, c0 = dst[sub]
                self.tr(self.banks[ob][:, c0:c0 + ncols], ot[0:ncols, sub * 128:(sub + 1) * 128], [("OT32", k)], [("bank", ob)],
                        ident=self.identf[0:ncols, 0:ncols])

    def softmax_pair(self, QT, KT, k_pr, q_pr, qb, kbs, mask, mshift, heads, Vres, SB4, T32, PW, maskfn=None, extra=None,
                     mask_key="cm", cshift=3, copy_eng="act"):
        banks, ident = self.banks, self.ident
        n = len(kbs)
        qsl = slice(qb * 512, (qb + 1) * 512)

        def A(i):
            kb = kbs[i]
            ksl = slice(kb * 128, (kb + 1) * 128)
            delta = 512 * qb - 128 * kb
            moff = maskfn(delta) if maskfn is not None else (delta + mshift if -384 <= delta <= 0 else None)
            ex = extra(kb) if extra is not None else None
            for h, H in enumerate(heads):
                bank, bp = SB4[i % len(SB4)][h], H["bp"]
                self.mm(banks[bank][:], KT[bp:bp + 64, k_pr, ksl], QT[bp:bp + 64, q_pr, qsl], True, moff is None and ex is None,
                        [(id(KT), k_pr, kb // 4), (id(QT), q_pr, qb)], [("bank", bank)])
            for h, H in enumerate(heads):
                bank = SB4[i % len(SB4)][h]
                if ex is not None:
                    self.mm(banks[bank][:], ex[0], ex[1], False, moff is None, ex[2], [("bank", bank)])
                if moff is not None:
                    self.mm(banks[bank][:], ident[:], mask[:, moff:moff + 512], False, True, ["ident", mask_key], [("bank", bank)])

        def B(i):
            kb = kbs[i]
            ci = (512 * qb - 128 * kb) // 128 + cshift
            for h, H in enumerate(heads):
                bank = SB4[i % len(SB4)][h]
                self.tt("dve", T32[i % len(T32)][h][:], banks[bank][:], H["abase"], ALU.add,
                        [("bank", bank), H["akeys"][1]], [("T32", i % len(T32), h)])
            for h, H in enumerate(heads):
                self.act(PW[i % 3][h][:], T32[i % len(T32)][h][:], AF.Exp, [("T32", i % len(T32), h), H["akeys"][0]], [("PW", i % 3, h)],
                         bias=H["acst"][:, ci:ci + 1])

        def C(i):
            kb = kbs[i]
            for h, H in enumerate(heads):
                for (vf, ncols, otb, doff) in H["pvT"]:
                    self.mm(banks[otb][0:ncols, :], vf(kb), PW[i % 3][h][:], i == 0, i == n - 1,
                            [("PW", i % 3, h), (id(Vres), kb)], [("bank", otb)])

        self.run_stages(n, [A, B, C])
        for H in heads:
            self.untranspose_multi([(otb, ncols, [(ob, c0 + doff) for (ob, c0) in H["oacc"]]) for (vf, ncols, otb, doff) in H["pvT"]],
                                   copy_eng)


    def softmax_block(self, QT, KT, pr, bp, qb, kbs, mask, mshift, abase, acst, cshift, Vres, vfn, dvp, oacc, starts, SB, T32, PW,
                      first=True, maskfn=None, extra=None, qt_pr=None, mask_key="cm",
                      delta_of=None, ci_of=None, kkey_of=None, vkey_of=None, acst_key="acst", abase_key="abase", pvT=None):
        banks, ident = self.banks, self.ident
        n = len(kbs)
        qsl = slice(qb * 512, (qb + 1) * 512)
        if qt_pr is None:
            qt_pr = pr
        kd = 128 if bp is None else 64
        b0 = 0 if bp is None else bp
        if delta_of is None:
            delta_of = lambda kb: 512 * qb - 128 * kb
        if ci_of is None:
            ci_of = lambda delta: delta // 128 + cshift
        if kkey_of is None:
            kkey_of = lambda kb: (id(KT), pr, kb // 4)
        if vkey_of is None:
            vkey_of = lambda kb: (id(Vres), kb)

        def A(i):
            kb = kbs[i]
            bank = SB[i % 2]
            ksl = slice(kb * 128, (kb + 1) * 128)
            delta = delta_of(kb)
            moff = maskfn(delta) if maskfn is not None else (delta + mshift if -384 <= delta <= 0 else None)
            ex = extra(kb) if extra is not None else None
            self.mm(banks[bank][:], KT[b0:b0 + kd, pr, ksl], QT[b0:b0 + kd, qt_pr, qsl], True, moff is None and ex is None,
                    [kkey_of(kb), (id(QT), qt_pr, qb)], [("bank", bank)])
            if ex is not None:
                self.mm(banks[bank][:], ex[0], ex[1], False, moff is None, ex[2], [("bank", bank)])
            if moff is not None:
                self.mm(banks[bank][:], ident[:], mask[:, moff:moff + 512], False, True, ["ident", mask_key], [("bank", bank)])

        def B(i):
            kb = kbs[i]
            bank = SB[i % 2]
            ci = ci_of(delta_of(kb))
            self.stt(T32[i % 2][:], banks[bank][:], acst[:, ci:ci + 1], abase, ALU.add, ALU.add,
                     [("bank", bank), acst_key, abase_key], [("T32", i % 2, 0)])
            self.act(PW[i % 3][:], T32[i % 2][:], AF.Exp, [("T32", i % 2, 0)], [("PW", i % 3, 0)])

        def C(i):
            kb = kbs[i]
            pw = PW[i % 3]
            if pvT is not None:
                for (vf, ncols, otb, doff) in pvT:
                    self.mm(banks[otb][0:ncols, :], vf(kb), pw[:], i == 0, i == n - 1,
                            [("PW", i % 3, 0), vkey_of(kb)], [("bank", otb)])
                return
            for sub in range(4):
                ob, c0 = oacc[sub]
                self.mm(banks[ob][:, c0:c0 + dvp], pw[:, sub * 128:(sub + 1) * 128], vfn(kb),
                        first and i == 0 and sub in starts, i == n - 1,
                        [("PW", i % 3, 0), vkey_of(kb)], [("bank", ob)], skip=True)

        self.run_stages(n, [A, B, C])
        if pvT is not None:
            for (vf, ncols, otb, doff) in pvT:
                self.untranspose(otb, ncols, [(ob, c0 + doff) for (ob, c0) in oacc], "act")


    def l3_pair(self, p, qb, QT, KT, V, ZS, cms, tri, onesn, E32, SP, MACC, EC, AW):
        banks, ident = self.banks, self.ident
        BA = (0, 1)
        BB = ((2, 3), (4, 5))
        OT = (6, 7)
        kbs = list(range(4 * qb + 3, -1, -1))
        n = len(kbs)
        qsl = slice(qb * 512, (qb + 1) * 512)

        def A(i):
            kb = kbs[i]
            ksl = slice(kb * 128, (kb + 1) * 128)
            j = kb - 4 * qb
            for h in range(2):
                bp = 64 * h
                self.mm(banks[BA[h]][:], KT[bp:bp + 64, p, ksl], QT[bp:bp + 64, p, qsl], True, j < 0,
                        [(id(KT), p, kb // 4), (id(QT), p, qb)], [("bank", BA[h])])
            if j >= 0:
                off = 384 - 128 * j
                for h in range(2):
                    self.mm(banks[BA[h]][:], ident[:], cms[:, off:off + 512], False, True, ["ident", "cms"], [("bank", BA[h])])

        def B(i):
            for h in range(2):
                self.act(E32[h][i % 3], banks[BA[h]][:], AF.Exp, [("bank", BA[h])], [("E32", h, i % 3)])
            for h in range(2):
                self.act(SP[h][i % 2], E32[h][i % 3], AF.Ln, [("E32", h, i % 3)], [("SP", h, i % 2)], bias=1.0)
            for h in range(2):
                bb = BB[i % 2][h]
                self.mm(banks[bb][:], tri[:], SP[h][i % 2], True, i == 0, ["tri", ("SP", h, i % 2)], [("bank", bb)])
                if i > 0:
                    self.mm(banks[bb][:], onesn[:], MACC[h][(i - 1) % 2][:], False, True,
                            ["onesn", ("MACC", h, (i - 1) % 2)], [("bank", bb)])
            if i < n - 1:
                for h in range(2):
                    if i == 0:
                        self.cp("dve", MACC[h][0][:], SP[h][0], [("SP", h, 0)], [("MACC", h, 0)])
                    else:
                        self.tt("dve", MACC[h][i % 2][:], MACC[h][(i - 1) % 2][:], SP[h][i % 2], ALU.add,
                                [("SP", h, i % 2), ("MACC", h, (i - 1) % 2)], [("MACC", h, i % 2)])

        def B2(i):
            for h in range(2):
                bb = BB[i % 2][h]
                self.act(EC[h][i % 2][:], banks[bb][:], AF.Exp, [("bank", bb)], [("EC", h, i % 2)])
            for h in range(2):
                self.tt("dve", AW[h][i % 2][:], E32[h][i % 3], EC[h][i % 2][:], ALU.mult,
                        [("E32", h, i % 3), ("EC", h, i % 2)], [("AW", h, i % 2)])

        def C(i):
            kb = kbs[i]
            for h in range(2):
                hd = 2 * p + h
                self.mm(banks[OT[h]][0:64, :], V[:, kb, hd * 64:(hd + 1) * 64], AW[h][i % 2][:], i == 0, i == n - 1,
                        [("AW", h, i % 2), ("V", kb)], [("bank", OT[h])])

        for step in range(n + 3):
            if 0 <= step - 1 < n:
                B(step - 1)
            if step < n:
                A(step)
            if 0 <= step - 2 < n:
                B2(step - 2)
            if 0 <= step - 3 < n:
                C(step - 3)
        for h in range(2):
            hd = 2 * p + h
            self.untranspose_multi([(OT[h], 64, [(OT[h], 64 * k) for k in range(4)])], "dve")
            for sub in range(4):
                tq = qb * 4 + sub
                zs = ZS[:, tq, hd * 64:(hd + 1) * 64]
                self.tt("dve", zs, banks[OT[h]][:, sub * 64:(sub + 1) * 64], zs, ALU.mult, [("bank", OT[h]), ("ZS", tq)], [("ZS", tq)])


    def l3_block(self, hd, p, bp, qb, ob, QT, KT, V, ZS, cms, tri, onesn, E32, SP, MACC, AW, BA, BB, EC):
        banks, ident = self.banks, self.ident
        kbs = list(range(4 * qb + 3, -1, -1))
        n = len(kbs)
        qsl = slice(qb * 512, (qb + 1) * 512)
        qkeys = [(id(QT), p, qb)]

        def scores(bank, kb, last_extra):
            ksl = slice(kb * 128, (kb + 1) * 128)
            j = kb - 4 * qb
            diag = j >= 0
            self.mm(banks[bank][:], KT[bp:bp + 64, p, ksl], QT[bp:bp + 64, p, qsl], True, (not diag and not last_extra),
                    [(id(KT), p, kb // 4)] + qkeys, [("bank", bank)])
            if diag:
                off = 384 - 128 * j
                self.mm(banks[bank][:], ident[:], cms[:, off:off + 512], False, not last_extra, ["ident", "cms"], [("bank", bank)])

        def A(i):
            scores(BA[i % 2], kbs[i], False)

        def B(i):
            kb = kbs[i]
            ba, bb = BA[i % 2], BB[i % 2]
            e, sp = E32[i % 2], SP[i % 3]
            self.act(e[:], banks[ba][:], AF.Exp, [("bank", ba)], [("E32", i % 2)])
            self.act(sp[:], e[:], AF.Ln, [("E32", i % 2)], [("SP", i % 3)], bias=1.0)
            self.mm(banks[bb][:], tri[:], sp[:], True, i == 0, ["tri", ("SP", i % 3)], [("bank", bb)])
            if i > 0:
                self.mm(banks[bb][:], onesn[:], MACC[(i - 1) % 2][:], False, True, ["onesn", ("MACC", (i - 1) % 2)], [("bank", bb)])
            if i < n - 1:
                if i == 0:
                    self.cp("dve", MACC[0][:], sp[:], [("SP", i % 3)], [("MACC", 0)])
                else:
                    self.tt("dve", MACC[i % 2][:], MACC[(i - 1) % 2][:], sp[:], ALU.add,
                            [("SP", i % 3), ("MACC", (i - 1) % 2)], [("MACC", i % 2)])

        def B2(i):
            bb = BB[i % 2]
            ec = EC[i % 2]
            self.act(ec[:], banks[bb][:], AF.Exp, [("bank", bb)], [("EC", i % 2)])
            self.tt("dve", AW[i % 2][:], E32[i % 2][:], ec[:], ALU.mult, [("E32", i % 2), ("EC", i % 2)], [("AW", i % 2)])

        def C(i):
            kb = kbs[i]
            aw = AW[i % 2]
            self.mm(banks[otb][0:64, :], V[:, kb, hd * 64:(hd + 1) * 64], aw[:], i == 0, i == n - 1,
                    [("AW", i % 2), ("V", kb)], [("bank", otb)])

        otb = self.l3_ot[(hd * 64 + qb) % 2]
        self.run_stages(n, [A, B, B2, C])
        self.untranspose(otb, 64, [(ob, 64 * k) for k in range(4)], "dve")
        for sub in range(4):
            tq = qb * 4 + sub
            zs = ZS[:, tq, hd * 64:(hd + 1) * 64]
            self.tt("dve", zs, banks[ob][:, sub * 64:(sub + 1) * 64], zs, ALU.mult, [("bank", ob), ("ZS", tq)], [("ZS", tq)])


def chunked(w):
    return np.ascontiguousarray(w.reshape(8, 128, w.shape[1]).transpose(1, 0, 2))


def rep128(v):
    return np.ascontiguousarray(np.broadcast_to(v[None, :], (128, v.shape[0]))).astype(np.float32)


_PROGS = {}


def get_prog(L, S, final=False, T=None, res=True, fused=False):
    key = (L, S, final, T, res, fused)
    if key not in _PROGS:
        _PROGS[key] = LayerProg(L, S, final=final, T=T, res=res, fused=fused)
    return _PROGS[key]


def consts_common():
    return {"ident": np.eye(128, dtype=np.float32).astype(NPBF), "identf": np.eye(128, dtype=np.float32)}


def layer_inputs(L, S, g, h, gpre, P, res=None):
    m = consts_common()
    m["h_in"] = None if h is None else np.ascontiguousarray(h, dtype=np.float32)
    m["gpre"] = rep128(gpre)
    if res is not None:
        ogT, wout, gpost = res
        m["ogT"] = ogT
        m["wout"] = chunked(wout)
        m["gpost"] = rep128(gpost)
    if L == 0:
        w = P["a_w_in"]
        NSEL, NT = S // 64, S // 128
        NC = S // 16
        NCT = max(1, NC // 128)
        NCP = NCT * 128
        NCMP = NC - 1
        hk = g
        c64 = lambda o: w[:, o + hk * 64:o + hk * 64 + 64]
        glc = np.concatenate([w[:, 2560 + br * 16 + 4 * hk:2560 + br * 16 + 4 * hk + 4] for br in range(3)], axis=1)
        m["win"] = chunked(np.concatenate([w[:, 256 * hk:256 * hk + 256], c64(1024), c64(1024), c64(1280), c64(1280),
                                           c64(1536), c64(1536), c64(2048), c64(2048), c64(1792), c64(2304), glc,
                                           w[:, 2608 + 256 * hk:2608 + 256 * hk + 256]], axis=1))
        for kv in "kv":
            m["w1" + kv] = np.ascontiguousarray(P["a_cmp_w1_" + kv].reshape(16, 128, 256).transpose(1, 0, 2))
            pos = P["a_cmp_pos_" + kv]
            m["pos2" + kv] = np.ascontiguousarray(pos.reshape(16, 2, 64).transpose(1, 2, 0).reshape(128, 1, 16))
        w2k = P["a_cmp_w2_k"].reshape(2, 128, 64).transpose(1, 0, 2)
        m["w2k"] = np.ascontiguousarray(np.concatenate([w2k, w2k], axis=2))
        m["w2v"] = np.ascontiguousarray(P["a_cmp_w2_v"].reshape(2, 128, 64).transpose(1, 0, 2))
        cidx = 16 * np.arange(NCMP)[:, None] + np.arange(32)[None, :]
        ov = np.zeros((NCP, NSEL + 1), np.float32)
        np.add.at(ov, (np.repeat(np.arange(NCMP), 32), (cidx // 64).ravel()), 1.0 / 32)
        ov[:, NSEL] = 1.0
        m["ovaug"] = np.ascontiguousarray(ov.reshape(NCT, 128, NSEL + 1).transpose(1, 0, 2)).astype(NPBF)
        m["cm"] = band_strip(896, 384, lambda d: d >= 0)
        m["wm"] = band_strip(1408, 384, lambda d: (d >= 0) & (d < 512))
        nn = np.arange(128)[:, None]
        xx = np.arange(2560)[None, :]
        m["cmk"] = np.where(xx - 16 * nn - 31 >= 0, 0.0, NEG).astype(NPBF)
        jj = np.arange(128)[:, None, None]
        kb_ = np.arange(NT)[None, :, None]
        kk_ = np.arange(128)[None, None, :]
        m["EM"] = np.where(jj == 2 * kb_ + kk_ // 64, 30000.0, 0.0).astype(NPBF)
        sl = alibi(16)[4 * hk:4 * hk + 4]
        kk = np.arange(128)[:, None]
        qq = np.arange(512)[None, :]
        m["abaseA"] = np.ascontiguousarray(np.stack([-s_ * (qq - kk) for s_ in sl], axis=1).astype(np.float32))
        m["abaseC"] = np.ascontiguousarray(np.stack([-s_ * (qq - 16 * kk) for s_ in sl], axis=1).astype(np.float32))
        ND = NT + 3
        dl = (np.arange(ND) - 3) * 128.0
        m["acstA"] = np.ascontiguousarray(np.broadcast_to(np.stack([-s_ * dl for s_ in sl], 0)[None], (128, 4, ND)).astype(np.float32))
        dc = np.arange(16) * 512.0 - 31.0
        m["acstC"] = np.ascontiguousarray(np.broadcast_to(np.stack([-s_ * dc for s_ in sl], 0)[None], (128, 4, 16)).astype(np.float32))
        qi = np.arange(128)[:, None]
        xa = np.arange(2 * NSEL)[None, :]
        dd = xa - (NSEL - 1) - qi // 64
        m["adj"] = np.where((dd == 0) | (dd == -1), 1000.0, np.where(dd > 0, -1.0, 0.0)).astype(np.float32)
    if L == 1:
        w = P["b_w_in"]
        cs = slice(256 * g, 256 * g + 256)
        m["win"] = chunked(np.concatenate([w[:, 0:1024][:, cs], w[:, 1024:2048][:, cs], w[:, 2048:3072][:, cs],
                                           w[:, 3072:4096][:, cs]], axis=1))
        m["cm"] = band_strip(896, 384, lambda d: d >= 0)
        sl = alibi(8)[2 * g:2 * g + 2]
        kk = np.arange(128)[:, None]
        qq = np.arange(512)[None, :]
        m["abase"] = np.ascontiguousarray(np.stack([-s_ * (qq - kk) for s_ in sl], axis=1).astype(np.float32))
        ND = S // 128 + 3
        dl = (np.arange(ND) - 3) * 128.0
        m["acst"] = np.ascontiguousarray(np.broadcast_to(np.stack([-s_ * dl for s_ in sl], 0)[None], (128, 2, ND)).astype(np.float32))
        m["lamr"] = np.ascontiguousarray(np.broadcast_to(P["b_lambda"][None], (128, 4, 64)).astype(np.float32))
        m["sgbc"] = rep128(P["b_sub_gain"])
    if L == 2:
        w = P["c_w_in"]
        PAT = ((128, 1), (512, 4), (2048, 16))
        qc = lambda gg: w[:, gg * 256 + g * 64: gg * 256 + g * 64 + 64]
        kc = lambda gg: w[:, 768 + gg * 256 + g * 64: 768 + gg * 256 + g * 64 + 64]
        m["win"] = chunked(np.concatenate([qc(0), qc(1), qc(2), kc(0), kc(1), kc(2), w[:, 1536 + g * 256:1536 + (g + 1) * 256],
                                           w[:, 2560 + g * 256:2560 + (g + 1) * 256]], axis=1))
        sl = alibi(12).reshape(3, 4)[:, g]
        kk = np.arange(128)[:, None]
        qq = np.arange(512)[None, :]
        m["abase"] = np.ascontiguousarray(np.stack([-s_ * (qq - kk) for s_ in sl], axis=1).astype(np.float32))
        dl = (np.arange(20) - 3) * 128.0
        m["acst"] = np.ascontiguousarray(np.broadcast_to(np.stack([-s_ * dl for s_ in sl], 0)[None], (128, 3, 20)).astype(np.float32))
        for gg, (wd, dil) in enumerate(PAT):
            m["dm%d" % gg] = band_strip(384 + wd + 512, 384, lambda d, wd=wd, dil=dil: (d >= 0) & (d <= wd) & (d % dil == 0))
    if L == 3:
        w = P["d_w_in"]
        cs = slice(256 * g, 256 * g + 256)
        m["win"] = chunked(np.concatenate([w[:, 0:1024][:, cs], w[:, 1024:2048][:, cs], w[:, 2048:3072][:, cs],
                                           w[:, 3072:4096][:, cs]], axis=1))
        m["cms"] = band_strip(896, 384, lambda d: d > 0)
        jj = np.arange(128)[:, None]
        ss = np.arange(128)[None, :]
        m["tri"] = np.where(jj >= ss, -1.0, 0.0).astype(NPBF)
        m["onesn"] = np.full((128, 128), -1.0, np.float32).astype(NPBF)
    return m


def kernel_unfused(x, norm_pre, norm_post, a_w_in, a_w_out, a_cmp_pos_k, a_cmp_pos_v, a_cmp_w1_k, a_cmp_w2_k,
           a_cmp_w1_v, a_cmp_w2_v, b_w_in, b_w_out, b_lambda, b_sub_gain, c_w_in, c_w_out, d_w_in, d_w_out):
    f = lambda a: np.asarray(a, dtype=np.float32)
    x, norm_pre, norm_post = f(x), f(norm_pre), f(norm_post)
    B, S, _ = x.shape
    P = {"a_w_in": f(a_w_in)[0], "a_cmp_pos_k": f(a_cmp_pos_k)[0], "a_cmp_pos_v": f(a_cmp_pos_v)[0],
         "a_cmp_w1_k": f(a_cmp_w1_k)[0], "a_cmp_w2_k": f(a_cmp_w2_k)[0], "a_cmp_w1_v": f(a_cmp_w1_v)[0],
         "a_cmp_w2_v": f(a_cmp_w2_v)[0], "b_w_in": f(b_w_in)[0], "b_lambda": f(b_lambda)[0],
         "b_sub_gain": f(b_sub_gain)[0], "c_w_in": f(c_w_in)[0], "d_w_in": f(d_w_in)[0]}
    wouts = [f(a_w_out)[0], f(b_w_out)[0], f(c_w_out)[0], f(d_w_out)[0]]
    h = [x[b] for b in range(B)]
    ogT = None
    cores = list(range(8))
    for L in range(4):
        prog = get_prog(L, S, res=(L > 0))
        maps = []
        for core in cores:
            b, g = divmod(core, 4)
            res = None if L == 0 else (ogT[b], wouts[L - 1], norm_post[L - 1])
            maps.append(layer_inputs(L, S, g, h[b], norm_pre[L], P, res))
        r = run_bass_kernel_spmd(prog.nc, maps, core_ids=cores).results
        if L > 0:
            h = [np.asarray(r[4 * b]["h_out"], dtype=np.float32) for b in range(B)]
        ogT = [np.ascontiguousarray(np.concatenate([np.asarray(r[4 * b + g]["og"]) for g in range(4)], axis=1).T)
               for b in range(B)]
    T = S // 4
    prog = get_prog(4, S, final=True, T=T)
    maps = []
    for core in cores:
        b, q = divmod(core, 4)
        sl = slice(q * T, (q + 1) * T)
        m = consts_common()
        m["h_in"] = np.ascontiguousarray(h[b][sl])
        m["ogT"] = np.ascontiguousarray(ogT[b][:, sl])
        m["wout"] = chunked(wouts[3])
        m["gpost"] = rep128(norm_post[3])
        maps.append(m)
    r = run_bass_kernel_spmd(prog.nc, maps, core_ids=cores).results
    out = np.stack([np.concatenate([np.asarray(r[4 * b + q]["h_out"], dtype=np.float32) for q in range(4)], axis=0)
                    for b in range(B)])
    return out.astype(np.float32)


def kernel(x, norm_pre, norm_post, a_w_in, a_w_out, a_cmp_pos_k, a_cmp_pos_v, a_cmp_w1_k, a_cmp_w2_k,
           a_cmp_w1_v, a_cmp_w2_v, b_w_in, b_w_out, b_lambda, b_sub_gain, c_w_in, c_w_out, d_w_in, d_w_out):
    f = lambda a: np.asarray(a, dtype=np.float32)
    x, norm_pre, norm_post = f(x), f(norm_pre), f(norm_post)
    B, S, _ = x.shape
    P = {"a_w_in": f(a_w_in)[0], "a_cmp_pos_k": f(a_cmp_pos_k)[0], "a_cmp_pos_v": f(a_cmp_pos_v)[0],
         "a_cmp_w1_k": f(a_cmp_w1_k)[0], "a_cmp_w2_k": f(a_cmp_w2_k)[0], "a_cmp_w1_v": f(a_cmp_w1_v)[0],
         "a_cmp_w2_v": f(a_cmp_w2_v)[0], "b_w_in": f(b_w_in)[0], "b_lambda": f(b_lambda)[0],
         "b_sub_gain": f(b_sub_gain)[0], "c_w_in": f(c_w_in)[0], "d_w_in": f(d_w_in)[0]}
    wouts = [f(a_w_out)[0], f(b_w_out)[0], f(c_w_out)[0], f(d_w_out)[0]]
    prog = get_prog(0, S, fused=True)
    cores = list(range(8))
    maps = []
    for core in cores:
        b, g = divmod(core, 4)
        m = dict(consts_common())
        m["x"] = np.ascontiguousarray(x[b])
        for L in range(4):
            lm = layer_inputs(L, S, g, None, norm_pre[L], P, None)
            lm.pop("h_in")
            lm.pop("ident")
            lm.pop("identf")
            if L > 0:
                lm["wout"] = chunked(wouts[L - 1])
                lm["gpost"] = rep128(norm_post[L - 1])
            for k, v in lm.items():
                m["L%d_%s" % (L, k)] = v
        m["F_wout"] = chunked(wouts[3])
        m["F_gpost"] = rep128(norm_post[3])
        maps.append(m)
    r = run_bass_kernel_spmd(prog.nc, maps, core_ids=cores).results
    return np.stack([np.asarray(r[4 * b]["out"], dtype=np.float32) for b in range(B)]).astype(np.float32)
```

```python
import math
from contextlib import ExitStack

import numpy as np
import ml_dtypes
import concourse.bass as bass
import concourse.mybir as mybir
from concourse.bass_utils import run_bass_kernel_spmd

F32 = mybir.dt.float32
BF16 = mybir.dt.bfloat16
AF = mybir.ActivationFunctionType
ALU = mybir.AluOpType
NPBF = ml_dtypes.bfloat16

D = 1024
NEG = -30000.0
EPS = 1e-6


class Sched:
    ENG = ("pe", "act", "dve", "pool", "sp")

    def __init__(self, nc):
        self.nc = nc
        self.eng = dict(pe=nc.tensor, act=nc.scalar, dve=nc.vector, pool=nc.gpsimd, sp=nc.sync)
        self.ops = []
        self.last_w = {}
        self.readers = {}
        self.bar = set()
        self.last_on = {}

    def add(self, eng, fn, kw, reads=(), writes=(), dma=None, inc=16):
        i = len(self.ops)
        bk = [r for r in reads if isinstance(r, tuple) and r[0] == "bank"]
        if bk:
            reads = [r for r in reads if not (isinstance(r, tuple) and r[0] == "bank")]
            writes = list(writes) + bk
        deps = set(self.bar)
        for r in reads:
            w = self.last_w.get(r)
            if w is not None:
                deps.add(w)
        for w in writes:
            lw = self.last_w.get(w)
            if lw is not None:
                deps.add(lw)
            rd = self.readers.get(w)
            if rd:
                deps.update(rd.values())
        self.ops.append([eng, (fn, kw), deps, dma, None, inc])
        for w in writes:
            self.last_w[w] = i
            self.readers[w] = {}
        for r in reads:
            self.readers.setdefault(r, {})[(eng, dma) if dma is not None else eng] = i
        self.last_on[(eng, dma)] = i
        return i

    def barrier(self):
        self.bar = set(v for (e, k), v in self.last_on.items() if not (isinstance(k, tuple) and k and k[0] == "cc"))

    def emit(self):
        nc = self.nc
        ops = self.ops
        needed = [False] * len(ops)
        for op in ops:
            for d in op[2]:
                dop = ops[d]
                if dop[3] is None and op[3] is None and dop[0] == "pe" and op[0] == "pe":
                    continue
                needed[d] = True
        esem = {e: nc.alloc_semaphore(name="sem_" + e) for e in self.ENG}
        dsem = {}
        cnt = {e: 0 for e in self.ENG}
        dcnt = {}
        for i, op in enumerate(ops):
            if op[3] is not None:
                k = op[3]
                if k not in dsem:
                    dsem[k] = nc.alloc_semaphore(name="dma_%d" % len(dsem))
                    dcnt[k] = 0
                dcnt[k] += op[5]
                op[4] = (k, dsem[k], dcnt[k])
            elif needed[i]:
                cnt[op[0]] += 1
                op[4] = (op[0], esem[op[0]], cnt[op[0]])
        seen = {e: {} for e in self.ENG}
        for i, op in enumerate(ops):
            e = op[0]
            E = self.eng[e]
            waits = {}
            for d in op[2]:
                dop = ops[d]
                if dop[3] is None and op[3] is None and dop[0] == "pe" and e == "pe":
                    continue
                key, sem, val = dop[4]
                if key not in waits or waits[key][1] < val:
                    waits[key] = (sem, val)
            for key, (sem, val) in waits.items():
                if seen[e].get(key, 0) >= val:
                    continue
                E.wait_ge(sem, val)
                seen[e][key] = val
            ins = op[1][0](**op[1][1])
            if op[4] is not None:
                ins.then_inc(op[4][1], op[5] if op[3] is not None else 1)
        for k, sem in dsem.items():
            nc.sync.wait_ge(sem, dcnt[k])
        self.n_ops = len(ops)


class Arena:
    def __init__(self, nc):
        self.nc = nc
        base = (nc.sbuf_base + 63) // 64 * 64
        size = nc.sbuf_top - base - 64
        self.slab = nc.alloc_sbuf_tensor("arena", [128, size], mybir.dt.uint8)
        self.base = base
        self.top = base
        self.end = base + size
        self.n = 0

    def mark(self):
        return self.top

    def reset(self, m):
        self.top = m

    def alloc(self, name, shape, dtype):
        nbytes = int(np.prod(shape[1:])) * (4 if dtype == F32 else 2)
        off = (self.top + 63) // 64 * 64
        assert off + nbytes <= self.end, "SBUF overflow at %s: need %d, have %d" % (name, nbytes, self.end - off)
        self.top = off + nbytes
        self.n += 1
        return self.nc.alloc_sbuf_tensor_at("%s_%d" % (name, self.n), list(shape), dtype, offset=off)


def alibi(n):
    return np.array([2.0 ** (-8.0 * (i + 1) / n) for i in range(n)], np.float32)


def band_strip(width, shift, pred):
    k = np.arange(128)[:, None]
    x = np.arange(width)[None, :]
    d = x - shift - k
    return np.where(pred(d), 0.0, NEG).astype(NPBF)


class LayerProg:
    def __init__(self, L, S, final=False, T=None, res=True, fused=False):
        self.L, self.S, self.final = L, S, final
        self.res = res or final
        self.fused = fused
        self.pfx = ""
        self.T = T if final else S
        nc = self.nc = bass.Bass("TRN2", target_bir_lowering=False)
        self.sc = Sched(nc)
        self.ar = Arena(nc)
        self.banks = [nc.alloc_psum_tensor("bank%d" % i, [128, 512], F32) for i in range(8)]
        self.din = {}
        self.dout = {}
        self.build()
        self.sc.emit()

    def mm(self, out, lhsT, rhs, start, stop, reads, writes, skip=False):
        kw = dict(out=out, lhsT=lhsT, rhs=rhs, start=start, stop=stop)
        if skip:
            kw["skip_group_check"] = True
        self.sc.add("pe", self.nc.tensor.matmul, kw, reads, writes)

    def tr(self, out, in_, reads, writes, ident=None):
        if ident is None:
            ident = self.ident[:]
        self.sc.add("pe", self.nc.tensor.transpose, dict(out=out, in_=in_, identity=ident), list(reads) + ["ident"], writes)

    def untranspose(self, otbank, ncols, dst, eng):
        k = self.ot_i % 2
        self.ot_i += 1
        ot = self.OT32[k]
        if eng == "act":
            self.act(ot[0:ncols, :], self.banks[otbank][0:ncols, :], AF.Copy, [("bank", otbank)], [("OT32", k)])
        else:
            self.cp("dve", ot[0:ncols, :], self.banks[otbank][0:ncols, :], [("bank", otbank)], [("OT32", k)])
        for sub in range(4):
            ob, c0 = dst[sub]
            self.tr(self.banks[ob][:, c0:c0 + ncols], ot[0:ncols, sub * 128:(sub + 1) * 128], [("OT32", k)], [("bank", ob)],
                    ident=self.identf[0:ncols, 0:ncols])

    def act(self, out, in_, func, reads, writes, **kw):
        self.sc.add("act", self.nc.scalar.activation, dict(out=out, in_=in_, func=func, **kw), reads, writes)

    def veng(self, eng):
        return self.nc.vector if eng == "dve" else self.nc.gpsimd

    def tt(self, eng, out, in0, in1, op, reads, writes):
        self.sc.add(eng, self.veng(eng).tensor_tensor, dict(out=out, in0=in0, in1=in1, op=op), reads, writes)

    def stt(self, out, in0, scalar, in1, op0, op1, reads, writes):
        self.sc.add("dve", self.nc.vector.scalar_tensor_tensor,
                    dict(out=out, in0=in0, scalar=scalar, in1=in1, op0=op0, op1=op1), reads, writes)

    def ts(self, eng, out, in0, s1, s2, op0, op1, reads, writes):
        kw = dict(out=out, in0=in0, scalar1=s1, scalar2=s2, op0=op0)
        if op1 is not None:
            kw["op1"] = op1
        self.sc.add(eng, self.veng(eng).tensor_scalar, kw, reads, writes)

    def cp(self, eng, out, in_, reads, writes):
        if eng == "act":
            self.sc.add("act", self.nc.scalar.copy, dict(out=out, in_=in_), reads, writes)
        else:
            self.sc.add(eng, self.veng(eng).tensor_copy, dict(out=out, in_=in_), reads, writes)

    def dma(self, eng, out, in_, reads, writes, key):
        E = self.nc.sync if eng == "sp" else self.nc.gpsimd
        self.sc.add(eng, E.dma_start, dict(out=out, in_=in_), reads, writes, dma=key)

    def inp(self, name, shape, dtype):
        name = self.pfx + name
        t = self.nc.dram_tensor(name, list(shape), dtype, kind="ExternalInput").ap()
        self.din[name] = t
        return t

    def outp(self, name, shape, dtype):
        t = self.nc.dram_tensor(name, list(shape), dtype, kind="ExternalOutput").ap()
        self.dout[name] = t
        return t

    def load_const(self, name, shape, dtype):
        src = self.inp(name, shape, dtype)
        dst = self.ar.alloc(name, shape, dtype)
        self.dma("sp", dst[:], src, [], [name], name)
        return dst

    def load_cast(self, name, shape):
        src = self.inp(name, shape, F32)
        dst = self.ar.alloc(name, shape, BF16)
        self.lc_i = getattr(self, "lc_i", 0)
        for c in range(shape[1]):
            i = self.lc_i % 2
            self.lc_i += 1
            st = self.wstage[i]
            keys = self.stage_keys[i]
            self.dma("sp", st[:, 0:shape[2]], src[:, c, :], [], keys, ("wstage", i))
            self.cp("pool", dst[:, c, :], st[:, 0:shape[2]], keys, [(name, c)])
        return dst

    def bank_bf16(self, i):
        return self.banks[i][:].bitcast(BF16)

    def run_stages(self, n, stages):
        for step in range(n + len(stages) - 1):
            for k, fn in enumerate(stages):
                i = step - k
                if 0 <= i < n:
                    fn(i)

    def prologue(self, fm_groups, tm_groups):
        nc, sc, ar, T = self.nc, self.sc, self.ar, self.T
        NT = T // 128
        has_res = self.res
        final = self.final
        banks = self.banks
        fused = self.fused
        h_in = self.h_src if fused else self.inp("h_in", [T, D], F32)
        if not final:
            ub = [ar.alloc("ub", [128, D], BF16) for _ in range(2)]
            uT = [ar.alloc("uT", [128, 8, 512], BF16) for _ in range(2)]
            wstage = [u_[:].rearrange("p c t -> p (c t)").bitcast(F32) for u_ in uT]
            stage_keys = [[("uT", i, q) for q in range(4)] for i in range(2)]
        else:
            ws = [ar.alloc("wstage", [128, 1024], F32) for _ in range(2)]
            wstage = [w_[:] for w_ in ws]
            stage_keys = [[("wstage", i)] for i in range(2)]
        self.wstage, self.stage_keys = wstage, stage_keys
        if has_res:
            if fused:
                ogTs = [a_.rearrange("(c p) t -> p c t", p=128) for a_ in self.allg]
            else:
                ogT = self.inp("ogT", [D, T], BF16).rearrange("(c p) t -> p c t", p=128)
            woutb = self.load_cast("wout", [128, 8, D])
            gpost = self.load_const("gpost", [128, D], F32)
            h_out = self.h_dst if fused else self.outp("h_out", [T, D], F32)
            ogc = [ar.alloc("ogc", [128, 8, 128], BF16) for _ in range(2)]
        if not final:
            gpre = self.load_const("gpre", [128, D], F32)
            winb = self.winb = self.load_cast("win", [128, 8, self.ncols])
        hin = [ar.alloc("hin", [128, D], F32) for _ in range(3)]
        stt_ = [ar.alloc("st", [128, 8], F32) for _ in range(4)]
        YB = [(0, 1), (2, 3)]
        PT = 4
        PJ = [5, 6, 7] if has_res else [0, 1, 2, 3, 5, 6, 7]
        pj = [0]

        def rstd_ops(src, dst, st, key):
            self.act(st[:, dst:dst + 1], st[:, src:src + 1], AF.Ln, [key], [key], scale=1.0 / D, bias=EPS)
            self.act(st[:, dst:dst + 1], st[:, dst:dst + 1], AF.Exp, [key], [key], scale=-0.5)

        def P0(tt):
            hb = hin[tt % 3]
            hk = ("hin", tt % 3)
            self.dma("sp", hb[:], h_in[tt * 128:(tt + 1) * 128, :], [], [hk], hk)
            if has_res:
                oc = ogc[tt % 2]
                ok = ("ogc", tt % 2)
                if fused:
                    j, r = divmod(tt, 16)
                    self.dma("sp", oc[:], ogTs[j][:, :, r * 128:(r + 1) * 128], [("ogall", j)], [ok], ok)
                else:
                    self.dma("sp", oc[:], ogT[:, :, tt * 128:(tt + 1) * 128], [], [ok], ok)

        def P1(tt):
            if has_res:
                oc = ogc[tt % 2]
                ok = ("ogc", tt % 2)
                yb = YB[tt % 2]
                for half in range(2):
                    for c in range(8):
                        self.mm(banks[yb[half]][:], oc[:, c, :], woutb[:, c, half * 512:(half + 1) * 512],
                                c == 0, c == 7, [ok, ("wout", c)], [("bank", yb[half])])

        def P2a(tt):
            if not has_res:
                return
            hb = hin[tt % 3]
            hk = ("hin", tt % 3)
            st = stt_[tt % 4]
            sk = ("st", tt % 4)
            yb = YB[tt % 2]
            jk = ogc[tt % 2][:].rearrange("p c t -> p (c t)")
            for half in range(2):
                hs = slice(half * 512, (half + 1) * 512)
                self.act(jk[:, hs], banks[yb[half]][:], AF.Square, [("bank", yb[half])], [sk, ("ogc", tt % 2)],
                         accum_out=st[:, half:half + 1])
            self.tt("dve", st[:, 2:3], st[:, 0:1], st[:, 1:2], ALU.add, [sk], [sk])
            rstd_ops(2, 3, st, sk)
            for half in range(2):
                hs = slice(half * 512, (half + 1) * 512)
                yp = banks[yb[half]][:]
                self.stt(yp, yp, st[:, 3:4], gpost[:, hs], ALU.mult, ALU.mult, [("bank", yb[half]), sk, "gpost"], [])
                self.tt("dve", hb[:, hs], yp, hb[:, hs], ALU.add, [("bank", yb[half]), hk], [hk])
            self.dma("pool", h_out[tt * 128:(tt + 1) * 128, :], hb[:], [hk], [], ("hst", tt % 3))

        def P2b(tt):
            if final:
                return
            hb = hin[tt % 3]
            hk = ("hin", tt % 3)
            st = stt_[tt % 4]
            sk = ("st", tt % 4)
            self.act(ub[tt % 2][:], hb[:], AF.Square, [hk], [sk, ("ub", tt % 2)], accum_out=st[:, 4:5])
            rstd_ops(4, 5, st, sk)
            self.stt(ub[tt % 2][:], hb[:], st[:, 5:6], gpre[:], ALU.mult, ALU.mult, [hk, sk, "gpre"], [("ub", tt % 2)])

        def P3(tt):
            if final:
                return
            ch, s = divmod(tt, 4)
            u = ub[tt % 2]
            pt = self.bank_bf16(PT)
            for c in range(8):
                self.tr(pt[:, c * 128:(c + 1) * 128], u[:, c * 128:(c + 1) * 128], [("ub", tt % 2)], [("bank", PT)])
            ut = uT[ch % 2]
            self.cp("dve", ut[:, :, s * 128:(s + 1) * 128], pt[:, 0:1024].rearrange("p (c t) -> p c t", c=8),
                    [("bank", PT)], [("uT", ch % 2, s)])

        def P4(tt):
            if final:
                return
            ch, s = divmod(tt, 4)
            ut = uT[ch % 2]
            for (off, n, evac) in tm_groups:
                bk = PJ[pj[0] % len(PJ)]
                pj[0] += 1
                for c in range(8):
                    self.mm(banks[bk][:, 0:n], ut[:, c, s * 128:(s + 1) * 128], winb[:, c, off:off + n], c == 0, c == 7,
                            [("uT", ch % 2, s), ("win", c)], [("bank", bk)])
                evac(banks[bk][:, 0:n], ("bank", bk), tt)
            if s == 3:
                for (off, n, evac) in fm_groups:
                    bk = PJ[pj[0] % len(PJ)]
                    pj[0] += 1
                    for c in range(8):
                        self.mm(banks[bk][0:n, :], winb[:, c, off:off + n], ut[:, c, :], c == 0, c == 7,
                                [("uT", ch % 2, q) for q in range(4)] + [("win", c)], [("bank", bk)])
                    evac(banks[bk][0:n, :], ("bank", bk), ch)

        stages = [P1, P2a, P2b, P3, P4]
        for step in range(NT + len(stages) + 1):
            for k, fn in enumerate(stages):
                i = step - 1 - k
                if 0 <= i < NT:
                    fn(i)
            if step < NT:
                P0(step)

    def build(self):
        self.ident = self.load_const("ident", [128, 128], BF16)
        self.identf = self.load_const("identf", [128, 128], F32)
        self.ot_i = 0
        if self.fused:
            return self.build_fused()
        if self.final:
            self.prologue([], [])
            return
        getattr(self, "build_L%d" % self.L)()

    def build_fused(self):
        nc, sc, ar, S = self.nc, self.sc, self.ar, self.S
        self.xin = self.inp("x", [S, D], F32)
        self.hbuf = nc.dram_tensor("hbuf", [S, D], F32).ap()
        NJ = S // 2048
        self.loc = [nc.dram_tensor("ogloc%d" % j, [256, 2048], BF16).ap() for j in range(NJ)]
        self.allg = [nc.dram_tensor("ogall%d" % j, [1024, 2048], BF16).ap() for j in range(NJ)]
        base = ar.mark()
        for L in range(4):
            self.L, self.pfx, self.res, self.final = L, "L%d_" % L, L > 0, False
            self.h_src = self.xin if L <= 1 else self.hbuf
            self.h_dst = self.hbuf
            getattr(self, "build_L%d" % L)()
            sc.barrier()
            ar.reset(base)
        self.L, self.pfx, self.res, self.final, self.T = 4, "F_", True, True, S
        self.h_src = self.hbuf
        self.h_dst = self.outp("out", [S, D], F32)
        self.prologue([], [])

    def fm_evac(self, dst, pr, scale, p0=0, p1=128):
        def evac(ps, pk, ch):
            self.act(dst[p0:p1, pr, ch * 512:(ch + 1) * 512], ps[p0:p1, :], AF.Copy, [pk], [(id(dst), pr, ch)], scale=scale)
        return evac

    def batch_silu(self, ZS, NT):
        for t0 in range(0, NT, 4):
            t1 = min(NT, t0 + 4)
            ks = [("ZS", t) for t in range(t0, t1)]
            self.act(ZS[:, t0:t1, :], ZS[:, t0:t1, :], AF.Silu, ks, ks)

    def store_og(self, ZS, NT):
        if self.fused:
            return
        og = self.outp("og", [self.S, 256], BF16)
        ogv = og.rearrange("(t p) f -> p t f", p=128)
        for t0 in range(0, NT, 16):
            t1 = min(NT, t0 + 16)
            self.dma("pool", ogv[:, t0:t1, :], ZS[:, t0:t1, :], [("ZS", t) for t in range(t0, t1)], [], ("ogst", t0))

    def ogT_begin(self):
        if self.fused:
            self.ogstg = [self.ar.alloc("ogstg", [128, 2, 512], BF16) for _ in range(2)]

    def ogT_qb(self, ZS, qb):
        if not self.fused:
            return
        nc, sc = self.nc, self.sc
        TBK = 7
        tb = self.bank_bf16(TBK)
        st = self.ogstg[qb % 2]
        for s4 in range(4):
            tt = 4 * qb + s4
            for fh in range(2):
                self.tr(tb[:, fh * 128:(fh + 1) * 128], ZS[:, tt, fh * 128:(fh + 1) * 128], [("ZS", tt)], [("bank", TBK)])
            self.cp("dve", st[:, :, s4 * 128:(s4 + 1) * 128], tb[:, 0:256].rearrange("p (h t) -> p h t", h=2),
                    [("bank", TBK)], [("ogstg", qb % 2, s4)])
        j, k = divmod(qb, 4)
        dst = self.loc[j].rearrange("(h p) t -> p h t", p=128)[:, :, k * 512:(k + 1) * 512]
        self.dma("pool", dst, st[:], [("ogstg", qb % 2, q) for q in range(4)], [("ogloc", j, k)], ("ogst", qb % 2))
        if k == 3:
            sc.add("pool", nc.gpsimd.collective_compute,
                   dict(kind="AllGather", op=ALU.bypass, replica_groups=[[0, 1, 2, 3], [4, 5, 6, 7]],
                        ins=[self.loc[j].opt()], outs=[self.allg[j].opt()]),
                   [("ogloc", j, q) for q in range(4)], [("ogall", j)], dma=("cc", j), inc=1)

    def build_L3(self):
        nc, sc, ar, S = self.nc, self.sc, self.ar, self.S
        NT = S // 128
        NQB = S // 512
        banks = self.banks
        self.ncols = 1024
        QT = ar.alloc("QT", [128, 2, S], BF16)
        KT = ar.alloc("KT", [128, 2, S], BF16)
        V = ar.alloc("V", [128, NT, 256], BF16)
        ZS = ar.alloc("ZS", [128, NT, 256], BF16)
        cms = self.load_const("cms", [128, 896], BF16)
        tri = self.load_const("tri", [128, 128], BF16)
        onesn = self.load_const("onesn", [128, 128], BF16)

        def tm_evac(ps, pk, tt):
            self.cp("dve", V[:, tt, :], ps[:, 0:256], [pk], [("V", tt)])
            self.act(ZS[:, tt, :], ps[:, 256:512], AF.Copy, [pk], [("ZS", tt)])

        fmg = [(0, 128, self.fm_evac(QT, 0, 0.125)), (128, 128, self.fm_evac(QT, 1, 0.125)),
               (256, 128, self.fm_evac(KT, 0, 1.0)), (384, 128, self.fm_evac(KT, 1, 1.0))]
        mark = ar.mark()
        self.prologue(fmg, [(512, 512, tm_evac)])
        sc.barrier()
        ar.reset(mark)
        self.alloc_ot()
        self.ogT_begin()
        self.batch_silu(ZS, NT)
        E32p = [ar.alloc("E32", [128, 2, 512], F32) for _ in range(3)]
        SPp = [ar.alloc("SP", [128, 2, 512], BF16) for _ in range(2)]
        E32 = [[E32p[k][:, h, :] for k in range(3)] for h in range(2)]
        SP = [[SPp[k][:, h, :] for k in range(2)] for h in range(2)]
        self.l3_pairbufs = (E32p, SPp)
        MACC = [[ar.alloc("MACC", [128, 512], BF16) for _ in range(2)] for _ in range(2)]
        EC = [[ar.alloc("EC", [128, 512], F32) for _ in range(2)] for _ in range(2)]
        AW = [[ar.alloc("AW", [128, 512], BF16) for _ in range(2)] for _ in range(2)]
        for p in range(2):
            for qb in range(NQB):
                self.l3_pair(p, qb, QT, KT, V, ZS, cms, tri, onesn, E32, SP, MACC, EC, AW)
                if p == 1:
                    self.ogT_qb(ZS, qb)
        self.store_og(ZS, NT)

    def build_L0(self):
        nc, sc, ar, S = self.nc, self.sc, self.ar, self.S
        NT, NQB, NSEL = S // 128, S // 512, S // 64
        NC = S // 16
        NCT = max(1, NC // 128)
        NCP = NCT * 128
        NCMP = NC - 1
        dvc = 65 + NSEL
        banks = self.banks
        self.ncols = 1164
        QT = ar.alloc("QT", [128, 2, S], BF16)
        KS2 = ar.alloc("KS2", [128, 1, S], BF16)
        KW2 = ar.alloc("KW2", [128, 1, S], BF16)
        VS = ar.alloc("VS", [128, NT, 65], BF16)
        VW = ar.alloc("VW", [128, NT, 65], BF16)
        ZS = ar.alloc("ZS", [128, NT, 256], BF16)
        G = ar.alloc("G", [128, NT, 12], F32)
        KCM2 = ar.alloc("KCM2", [128, 1, NCP], BF16)
        VEXT = ar.alloc("VEXT", [128, NCT, dvc], BF16)
        sc.add("pool", nc.gpsimd.memset, dict(ap=VS[:, :, 64:65], constant=1.0), [], ["Vones"])
        sc.add("pool", nc.gpsimd.memset, dict(ap=VW[:, :, 64:65], constant=1.0), ["Vones"], ["Vones"])
        ovaug = self.inp("ovaug", [128, NCT, NSEL + 1], BF16)
        self.dma("sp", VEXT[:, :, 64:dvc], ovaug, [], ["ovaug"], "ovaug")
        mark1 = ar.mark()
        KC2 = ar.alloc("KC2", [128, S + 16], BF16)
        VC2 = ar.alloc("VC2", [128, S + 16], BF16)
        mark2 = ar.mark()

        def shift_evac(dst, name):
            def evac(ps, pk, ch):
                c0 = ch * 512
                self.act(dst[0:64, c0:c0 + 512], ps[0:64, :], AF.Copy, [pk], [(name, ch, 0)])
                if ch == 0:
                    self.act(dst[64:128, 0:511], ps[64:128, 1:512], AF.Copy, [pk], [(name, ch, 1)])
                else:
                    self.act(dst[64:128, c0 - 1:c0 + 511], ps[64:128, :], AF.Copy, [pk], [(name, ch, 1)])
            return evac

        def tm_evac(ps, pk, tt):
            self.cp("dve", VS[:, tt, 0:64], ps[:, 0:64], [pk, "Vones"], [(id(VS), tt)])
            self.cp("dve", VW[:, tt, 0:64], ps[:, 64:128], [pk, "Vones"], [(id(VW), tt)])
            self.act(G[:, tt, :], ps[:, 128:140], AF.Copy, [pk], [("G", tt)])
            self.act(ZS[:, tt, :], ps[:, 140:396], AF.Copy, [pk], [("ZS", tt)])

        fmg = [(0, 128, self.fm_evac(QT, 0, 0.125)), (128, 128, self.fm_evac(QT, 1, 0.125)),
               (256, 128, shift_evac(KC2, "KC2")), (384, 128, shift_evac(VC2, "VC2")),
               (512, 128, self.fm_evac(KS2, 0, 1.0)), (640, 128, self.fm_evac(KW2, 0, 1.0))]
        self.prologue(fmg, [(768, 396, tm_evac)])
        sc.barrier()
        ar.reset(mark2)
        wst = [ar.alloc("cwst", [128, 256], F32) for _ in range(2)]
        self.wstage = [w_[:] for w_ in wst]
        self.stage_keys = [[("cwst", i)] for i in range(2)]
        w1 = {kv: self.load_cast("w1" + kv, [128, 16, 256]) for kv in "kv"}
        w2k = self.load_cast("w2k", [128, 2, 128])
        w2v = self.load_cast("w2v", [128, 2, 64])
        pos2 = {kv: self.load_cast("pos2" + kv, [128, 1, 16]) for kv in "kv"}
        posb = ar.alloc("posb", [128, 4], F32)
        HID = {kv: ar.alloc("HID" + kv, [128, 2, NCP], BF16) for kv in "kv"}
        for kv in "kv":
            sc.add("pool", nc.gpsimd.memset, dict(ap=HID[kv][:], constant=0.0), [], [("HID" + kv, 0), ("HID" + kv, 1)])
        X32 = [ar.alloc("X32", [128, 512], F32) for _ in range(2)]
        X2 = [ar.alloc("X2", [128, 512], F32) for _ in range(2)]
        bkp = 6
        for qi, (kv, half) in enumerate([("k", 0), ("k", 1), ("v", 0), ("v", 1)]):
            for c in range(16):
                self.mm(banks[bkp][:, qi:qi + 1], w1[kv][:, c, half * 128:(half + 1) * 128], pos2[kv][:, 0, c:c + 1],
                        c == 0, c == 15, [("w1" + kv, c), ("pos2" + kv, 0)], [("bank", bkp)], skip=True)
        self.cp("dve", posb[:], banks[bkp][:, 0:4], [("bank", bkp)], ["posb"])
        src = {"k": KC2, "v": VC2}
        it = 0
        for kv in "kv":
            for half in range(2):
                bk = (0, 1)[it % 2]
                for c in range(16):
                    self.mm(banks[bk][:, 0:NCMP], w1[kv][:, c, half * 128:(half + 1) * 128],
                            src[kv][:, 2 * c:2 * c + 16 * (NCMP - 1) + 1:16], c == 0, c == 15, [("w1" + kv, c)], [("bank", bk)])
                x, x2 = X32[it % 2][:, 0:NCMP], X2[it % 2][:, 0:NCMP]
                xk, x2k = ("X32", it % 2), ("X2", it % 2)
                pcol = {"k": 0, "v": 2}[kv] + half
                self.ts("dve", x, banks[bk][:, 0:NCMP], posb[:, pcol:pcol + 1], None, ALU.add, None, [("bank", bk), "posb"], [xk])
                self.tt("pool", x2, x, x, ALU.mult, [xk], [x2k])
                self.ts("pool", x2, x2, 0.044715, 1.0, ALU.mult, ALU.add, [x2k], [x2k])
                self.tt("pool", x2, x2, x, ALU.mult, [xk, x2k], [x2k])
                self.act(x2, x2, AF.Sigmoid, [x2k], [x2k], scale=1.5957691216057308)
                self.tt("dve", HID[kv][:, half, 0:NCMP], x, x2, ALU.mult, [xk, x2k], [("HID" + kv, half)])
                it += 1
        for half in range(2):
            self.mm(banks[2][:, 0:NCP], w2k[:, half, :], HID["k"][:, half, :], half == 0, half == 1,
                    [("w2k", half), ("HIDk", half)], [("bank", 2)])
        self.cp("dve", KCM2[:, 0, :], banks[2][:, 0:NCP], [("bank", 2)], ["KCM2"])
        for j in range(NCT):
            bk = 3 + j % 2
            for half in range(2):
                self.mm(banks[bk][:, 0:64], HID["v"][:, half, j * 128:(j + 1) * 128], w2v[:, half, :], half == 0, half == 1,
                        [("HIDv", half), ("w2v", half)], [("bank", bk)])
            self.cp("dve", VEXT[:, j, 0:64], banks[bk][:, 0:64], [("bank", bk), "ovaug"], [("VEXT", j)])
        sc.barrier()
        ar.reset(mark1)
        self.alloc_ot()
        self.ogT_begin()
        self.batch_silu(ZS, NT)
        self.act(G[:].rearrange("p t g -> p (t g)"), G[:].rearrange("p t g -> p (t g)"), AF.Sigmoid,
                 [("G", t) for t in range(NT)], [("G", t) for t in range(NT)])
        ND = NT + 3
        cm = self.load_const("cm", [128, 896], BF16)
        wm = self.load_const("wm", [128, 1408], BF16)
        cmk = self.load_const("cmk", [128, 2560], BF16)
        EM = self.load_const("EM", [128, NT, 128], BF16)
        abaseA = self.load_const("abaseA", [128, 4, 512], F32)
        abaseC = self.load_const("abaseC", [128, 4, 512], F32)
        acstA = self.load_const("acstA", [128, 4, ND], F32)
        acstC = self.load_const("acstC", [128, 4, 16], F32)
        adj = self.load_const("adj", [128, 2 * NSEL], F32)
        T32 = [[ar.alloc("T32", [128, 512], F32) for _ in range(2)] for _ in range(2)]
        PW = [[ar.alloc("PW", [128, 512], BF16) for _ in range(2)] for _ in range(3)]
        T32c = [T32[0][0], T32[1][0]]
        PWc = [PW[0][0], PW[1][0], PW[2][0]]
        SB4 = ((0, 1), (2, 3), (4, 5))
        SELT = [ar.alloc("SELT", [128, 512], BF16) for _ in range(2)]
        for i in range(2):
            sc.add("pool", nc.gpsimd.memset, dict(ap=SELT[i][:], constant=0.0), [], [("SELT", i, q) for q in range(4)])
        ACC = [ar.alloc("ACC", [128, 4, 256], F32) for _ in range(2)]
        IMP = ar.alloc("IMP", [128, 4, NSEL], F32)
        M8 = [ar.alloc("M8", [128, 16], F32) for _ in range(2)]
        WK = [ar.alloc("WK", [128, NSEL], F32) for _ in range(2)]
        SR = [ar.alloc("SR", [128, NSEL], BF16) for _ in range(2)]
        fs = [ar.alloc("fs", [128, 12], F32) for _ in range(4)]
        SB = (0, 1)
        pairs = [(2, 3), (4, 5)]
        singles = [2, 3, 4, 5]
        TB = 6
        st8 = {"fi": 0, "si": 0}

        def fin_coef4(lsrc, qb_, gcol):
            f = fs[st8["fi"] % 4]
            fk = ("fs", st8["fi"] % 4)
            st8["fi"] += 1
            for sub, (ap_, bk_) in enumerate(lsrc):
                self.ts("dve", f[:, sub:sub + 1], ap_, 1e-30, None, ALU.max, None, [("bank", bk_)], [fk])
            sc.add("dve", nc.vector.reciprocal, dict(out=f[:, 4:8], in_=f[:, 0:4]), [fk], [fk])
            self.tt("dve", f[:, 8:12], f[:, 4:8], G[:, 4 * qb_:4 * qb_ + 4, gcol], ALU.mult,
                    [fk] + [("G", 4 * qb_ + q) for q in range(4)], [fk])
            return f, fk

        for qb in range(NQB):
            acc = ACC[qb % 2]
            selt = SELT[qb % 2]
            for g in range(4):
                pb = pairs[g % 2]
                oacc = [(pb[0], 0), (pb[0], dvc), (pb[1], 0), (pb[1], dvc)]
                kbs = [j for j in range(NCT) if 2048 * j + 31 <= 512 * qb + 511]
                self.softmax_block(QT, KCM2, 0, 64 * (g % 2), qb, kbs, cmk, 0, abaseC[:, g, :], acstC[:, g, :], 0, VEXT,
                                   lambda j: VEXT[:, j, :], dvc, oacc, (0, 2), SB, T32c, PWc, first=True,
                                   maskfn=lambda delta: delta if delta <= 2048 else None, qt_pr=g // 2, mask_key="cmk",
                                   delta_of=lambda j: 512 * qb - 2048 * j, ci_of=lambda delta: delta // 512,
                                   kkey_of=lambda j: "KCM2", vkey_of=lambda j: ("VEXT", j), acst_key="acstC", abase_key="abaseC")
                f, fk = fin_coef4([(banks[ob_][:, c0_ + dvc - 1:c0_ + dvc], ob_) for (ob_, c0_) in oacc], qb, g)
                for sub in range(4):
                    tq = 4 * qb + sub
                    ob, c0 = oacc[sub]
                    imp_ps = banks[ob][:, c0 + 64:c0 + 64 + NSEL]
                    if g == 0:
                        self.ts("dve", IMP[:, sub, :], imp_ps, f[:, 4 + sub:5 + sub], None, ALU.mult, None, [("bank", ob), fk], [("IMP", sub)])
                    else:
                        self.stt(IMP[:, sub, :], imp_ps, f[:, 4 + sub:5 + sub], IMP[:, sub, :], ALU.mult, ALU.add,
                                 [("bank", ob), fk, ("IMP", sub)], [("IMP", sub)])
                    self.ts("dve", acc[:, sub, g * 64:(g + 1) * 64], banks[ob][:, c0:c0 + 64], f[:, 8 + sub:9 + sub], None, ALU.mult, None,
                            [("bank", ob), fk], [("ACC", qb % 2, sub)])
            for sub in range(4):
                tq = 4 * qb + sub
                imp = IMP[:, sub, :]
                ik = ("IMP", sub)
                a0 = NSEL - 1 - 2 * tq
                self.tt("dve", imp, imp, adj[:, a0:a0 + NSEL], ALU.add, [ik, "adj"], [ik])
                if tq >= 1:
                    self.ts("dve", IMP[:, sub, 0:1], IMP[:, sub, 0:1], 1000.0, None, ALU.add, None, [ik], [ik])
                m8, wk, sr = M8[sub % 2], WK[sub % 2], SR[sub % 2]
                mk, wkk, srk = ("M8", sub % 2), ("WK", sub % 2), ("SR", sub % 2)
                sc.add("dve", nc.vector.max, dict(out=m8[:, 0:8], in_=imp), [ik], [mk])
                sc.add("dve", nc.vector.match_replace, dict(out=wk[:], in_to_replace=m8[:, 0:8], in_values=imp, imm_value=-1e9),
                       [ik, mk], [wkk])
                sc.add("dve", nc.vector.max, dict(out=m8[:, 8:16], in_=wk[:]), [wkk, mk], [mk])
                self.ts("dve", sr[:], imp, m8[:, 15:16], 1.0, ALU.is_ge, ALU.subtract, [ik, mk], [srk])
                tb = self.bank_bf16(TB)
                self.tr(tb[0:NSEL, sub * 128:(sub + 1) * 128], sr[:], [srk], [("bank", TB)])
                self.cp("dve", selt[0:NSEL, sub * 128:(sub + 1) * 128], tb[0:NSEL, sub * 128:(sub + 1) * 128],
                        [("bank", TB)], [("SELT", qb % 2, sub)])
            for br in (2, 1):
                for gp in range(2):
                    otbs = (6, 7)
                    heads = []
                    for h in range(2):
                        g = 2 * gp + h
                        Vb = VW if br == 2 else VS
                        heads.append(dict(bp=64 * h, abase=abaseA[:, g, :], acst=acstA[:, g, :], akeys=("acstA", "abaseA"),
                                          pvT=[(lambda kb, Vb=Vb: Vb[:, kb, :], 65, otbs[h], 0)],
                                          oacc=[(otbs[h], 65 * k) for k in range(4)]))
                    if br == 2:
                        kbs = list(range(4 * qb + 3, max(0, 4 * qb - 4) - 1, -1))
                        self.softmax_pair(QT, KW2, 0, gp, qb, kbs, wm, 384, heads, VW, SB4, T32, PW,
                                          maskfn=lambda delta: delta + 384, mask_key="wm")
                    else:
                        kbs = list(range(4 * qb + 3, -1, -1))
                        self.softmax_pair(QT, KS2, 0, gp, qb, kbs, cm, 384, heads, VS, SB4, T32, PW,
                                          extra=lambda kb: (EM[:, kb, :], selt[:], ["EM"] + [("SELT", qb % 2, q) for q in range(4)]))
                    for h in range(2):
                        g = 2 * gp + h
                        ob = otbs[h]
                        f, fk = fin_coef4([(banks[ob][:, 65 * q + 64:65 * q + 65], ob) for q in range(4)], qb, br * 4 + g)
                        for sub in range(4):
                            a_ = acc[:, sub, g * 64:(g + 1) * 64]
                            self.stt(a_, banks[ob][:, 65 * sub:65 * sub + 64], f[:, 8 + sub:9 + sub], a_, ALU.mult, ALU.add,
                                     [("bank", ob), fk, ("ACC", qb % 2, sub)], [("ACC", qb % 2, sub)])
            for sub in range(4):
                tq = 4 * qb + sub
                self.tt("dve", ZS[:, tq, :], acc[:, sub, :], ZS[:, tq, :], ALU.mult, [("ACC", qb % 2, sub), ("ZS", tq)], [("ZS", tq)])
            self.ogT_qb(ZS, qb)
        self.store_og(ZS, NT)


    def build_L1(self):
        nc, sc, ar, S = self.nc, self.sc, self.ar, self.S
        NT = S // 128
        NQB = S // 512
        ND = NT + 3
        banks, X = self.banks, mybir.AxisListType.X
        self.ncols = 1024
        QT = ar.alloc("QT", [128, 2, S], BF16)
        KT = ar.alloc("KT", [128, 2, S], BF16)
        V = ar.alloc("V", [128, NT, 258], BF16)
        ZS = ar.alloc("ZS", [128, NT, 256], BF16)
        self.sc.add("pool", nc.gpsimd.memset, dict(ap=V[:, :, 128:129], constant=1.0), [], ["Vones"])
        self.sc.add("pool", nc.gpsimd.memset, dict(ap=V[:, :, 257:258], constant=1.0), ["Vones"], ["Vones"])

        def tm_evac(ps, pk, tt):
            self.cp("dve", V[:, tt, 0:128], ps[:, 0:128], [pk, "Vones"], [(id(V), tt)])
            self.cp("dve", V[:, tt, 129:257], ps[:, 128:256], [pk, "Vones", (id(V), tt)], [(id(V), tt)])
            self.act(ZS[:, tt, :], ps[:, 256:512], AF.Copy, [pk], [("ZS", tt)])

        fmg = [(0, 128, self.fm_evac(QT, 0, 0.125)), (128, 128, self.fm_evac(QT, 1, 0.125)),
               (256, 128, self.fm_evac(KT, 0, 1.0)), (384, 128, self.fm_evac(KT, 1, 1.0))]
        mark = ar.mark()
        self.prologue(fmg, [(512, 512, tm_evac)])
        sc.barrier()
        ar.reset(mark)
        self.alloc_ot()
        self.ogT_begin()
        self.batch_silu(ZS, NT)
        cm = self.load_const("cm", [128, 896], BF16)
        abase = self.load_const("abase", [128, 2, 512], F32)
        acst = self.load_const("acst", [128, 2, ND], F32)
        lamr = self.load_const("lamr", [128, 4, 64], F32)
        sgbc = self.load_const("sgbc", [128, 128], F32)
        fs = [ar.alloc("fs", [128, 16], F32) for _ in range(2)]
        RES = [ar.alloc("RES", [128, 4, 129], F32) for _ in range(2)]
        a32 = [ar.alloc("a32", [128, 128], F32) for _ in range(2)]
        o32 = [ar.alloc("o32", [128, 128], F32) for _ in range(2)]
        jk = ar.alloc("jk", [128, 128], F32)
        lam = ar.alloc("lam", [128, 8], F32)
        li = 0.8 - 0.6 * math.exp(-0.3 * self.L)
        pr_ = ar.alloc("lprod", [128, 2, 64], F32)
        self.tt("dve", pr_[:, 0, :], lamr[:, 0, :], lamr[:, 1, :], ALU.mult, ["lamr"], ["lprod"])
        self.tt("dve", pr_[:, 1, :], lamr[:, 2, :], lamr[:, 3, :], ALU.mult, ["lamr", "lprod"], ["lprod"])
        sc.add("dve", nc.vector.tensor_reduce, dict(out=lam[:, 0:2], in_=pr_[:], axis=X, op=ALU.add), ["lprod"], ["lam"])
        self.act(lam[:, 2:4], lam[:, 0:2], AF.Exp, ["lam"], ["lam"])
        self.tt("dve", lam[:, 4:5], lam[:, 3:4], lam[:, 2:3], ALU.subtract, ["lam"], ["lam"])
        self.ts("dve", lam[:, 4:5], lam[:, 4:5], -li, None, ALU.add, None, ["lam"], ["lam"])
        SB4 = ((0, 1), (2, 3))
        OB = ((4, 5), (6, 7))
        T32 = [[ar.alloc("T32", [128, 512], F32) for _ in range(2)] for _ in range(2)]
        PW = [[ar.alloc("PW", [128, 512], BF16) for _ in range(2)] for _ in range(3)]
        fi = 0
        for hl in range(2):
            for qb in range(NQB):
                heads = []
                for m in range(2):
                    heads.append(dict(bp=64 * m, abase=abase[:, hl, :], acst=acst[:, hl, :], akeys=("acst", "abase"),
                                      pvT=[(lambda kb, hl=hl: V[:, kb, hl * 129:hl * 129 + 128], 128, OB[m][0], 0),
                                           (lambda kb, hl=hl: V[:, kb, hl * 129 + 128:hl * 129 + 129], 1, OB[m][1], 128)],
                                      oacc=[(OB[m][0], 0), (OB[m][0], 129), (OB[m][1], 0), (OB[m][1], 129)]))
                self.softmax_pair(QT, KT, hl, hl, qb, list(range(4 * qb + 3, -1, -1)), cm, 384, heads, V, SB4, T32, PW)
                for m in range(2):
                    for half in range(2):
                        bk_ = OB[m][half]
                        self.act(RES[m][:, 2 * half:2 * half + 2, :].rearrange("p s c -> p (s c)"), banks[bk_][:, 0:258], AF.Copy,
                                 [("bank", bk_)], [("RES", m, half)])
                for sub in range(4):
                    tq = qb * 4 + sub
                    f = fs[fi % 2]
                    fk = ("fs", fi % 2)
                    a_, o_ = a32[fi % 2], o32[fi % 2]
                    fi += 1
                    r0, r1 = RES[0][:, sub, :], RES[1][:, sub, :]
                    k0, k1 = ("RES", 0, sub // 2), ("RES", 1, sub // 2)
                    self.cp("dve", f[:, 0:1], r0[:, 128:129], [k0], [fk])
                    self.cp("dve", f[:, 1:2], r1[:, 128:129], [k1, fk], [fk])
                    sc.add("dve", nc.vector.reciprocal, dict(out=f[:, 2:4], in_=f[:, 0:2]), [fk], [fk])
                    self.tt("dve", f[:, 4:5], f[:, 3:4], lam[:, 4:5], ALU.mult, [fk, "lam"], [fk])
                    self.ts("dve", a_[:], r0[:, 0:128], f[:, 2:3], None, ALU.mult, None, [k0, fk], [("a32", fi % 2)])
                    self.stt(o_[:], r1[:, 0:128], f[:, 4:5], a_[:], ALU.mult, ALU.add, [k1, fk, ("a32", fi % 2)], [("o32", fi % 2)])
                    self.act(jk[:], o_[:], AF.Square, [("o32", fi % 2)], [fk], accum_out=f[:, 5:6])
                    self.act(f[:, 6:7], f[:, 5:6], AF.Ln, [fk], [fk], scale=1.0 / 128, bias=EPS)
                    self.act(f[:, 6:7], f[:, 6:7], AF.Exp, [fk], [fk], scale=-0.5, bias=math.log(1.0 - li))
                    self.stt(o_[:], o_[:], f[:, 6:7], sgbc[:], ALU.mult, ALU.mult, [fk, "sgbc", ("o32", fi % 2)], [("o32", fi % 2)])
                    zs = ZS[:, tq, hl * 128:(hl + 1) * 128]
                    self.tt("dve", zs, o_[:], zs, ALU.mult, [("o32", fi % 2), ("ZS", tq)], [("ZS", tq)])
                if hl == 1:
                    self.ogT_qb(ZS, qb)
        self.store_og(ZS, NT)

    def build_L2(self):
        nc, sc, ar, S = self.nc, self.sc, self.ar, self.S
        NT = S // 128
        NQB = S // 512
        banks = self.banks
        self.ncols = 896
        PAT = ((128, 1), (512, 4), (2048, 16))
        QT = ar.alloc("QT", [128, 2, S], BF16)
        KT = ar.alloc("KT", [128, 2, S], BF16)
        V = ar.alloc("V", [128, NT, 257], BF16)
        ZS = ar.alloc("ZS", [128, NT, 256], BF16)
        self.sc.add("pool", nc.gpsimd.memset, dict(ap=V[:, :, 256:257], constant=1.0), [], ["Vones"])

        def tm_evac(ps, pk, tt):
            self.cp("dve", V[:, tt, 0:256], ps[:, 0:256], [pk, "Vones"], [(id(V), tt)])
            self.act(ZS[:, tt, :], ps[:, 256:512], AF.Copy, [pk], [("ZS", tt)])

        fmg = [(0, 128, self.fm_evac(QT, 0, 0.125)), (128, 64, self.fm_evac(QT, 1, 0.125, 0, 64)),
               (192, 128, self.fm_evac(KT, 0, 1.0)), (320, 64, self.fm_evac(KT, 1, 1.0, 0, 64))]
        mark = ar.mark()
        self.prologue(fmg, [(384, 512, tm_evac)])
        sc.barrier()
        ar.reset(mark)
        self.ogT_begin()
        self.batch_silu(ZS, NT)
        ND = 20
        strips = [self.load_const("dm%d" % g, [128, 384 + w + 512], BF16) for g, (w, d) in enumerate(PAT)]
        abase = self.load_const("abase", [128, 3, 512], F32)
        acst = self.load_const("acst", [128, 3, ND], F32)
        T32 = [ar.alloc("T32", [128, 512], F32) for _ in range(2)]
        PW = [ar.alloc("PW", [128, 512], BF16) for _ in range(3)]
        fs = [ar.alloc("fs", [128, 4], F32) for _ in range(2)]
        SB = (0, 1)
        oacc = [(2, 0), (3, 0), (4, 0), (5, 0)]
        fi = 0
        for qb in range(NQB):
            for g, (w, d) in enumerate(PAT):
                pr, bp = ((0, 0), (0, 64), (1, 0))[g]
                kb_min = max(0, (512 * qb - w) // 128)
                kbs = list(range(4 * qb + 3, kb_min - 1, -1))
                self.softmax_block(QT, KT, pr, bp, qb, kbs, strips[g], 384, abase[:, g, :], acst[:, g, :], 3, V,
                                   lambda kb: V[:, kb, :], 257, oacc, (0, 1, 2, 3), SB, T32, PW, first=(g == 0),
                                   maskfn=lambda delta: delta + 384, mask_key="dm%d" % g)
            for sub in range(4):
                tq = qb * 4 + sub
                f = fs[fi % 2]
                fk = ("fs", fi % 2)
                fi += 1
                ob = oacc[sub][0]
                sc.add("dve", nc.vector.reciprocal, dict(out=f[:, 0:1], in_=banks[ob][:, 256:257]), [("bank", ob)], [fk])
                self.stt(ZS[:, tq, :], banks[ob][:, 0:256], f[:, 0:1], ZS[:, tq, :], ALU.mult, ALU.mult,
                         [("bank", ob), fk, ("ZS", tq)], [("ZS", tq)])
            self.ogT_qb(ZS, qb)
        self.store_og(ZS, NT)


    def alloc_ot(self):
        self.OT32 = [self.ar.alloc("OT32", [128, 512], F32) for _ in range(2)]

    def untranspose_multi(self, chunks, eng):
        cps = []
        for (otbank, ncols, dst) in chunks:
            k = self.ot_i % len(self.OT32)
            self.ot_i += 1
            ot = self.OT32[k]
            if eng == "act":
                self.act(ot[0:ncols, :], self.banks[otbank][0:ncols, :], AF.Copy, [("bank", otbank)], [("OT32", k)])
            else:
                self.cp("dve", ot[0:ncols, :], self.banks[otbank][0:ncols, :], [("bank", otbank)], [("OT32", k)])
            cps.append((ot, k, ncols, dst))
        for (ot, k, ncols, dst) in cps:
            for sub in range(4):
                ob, c0 = dst[sub]
                self.tr(self.banks[ob][:, c0:c0 + ncols], ot[0:ncols, sub * 128:(sub + 1) * 128], [("OT32", k)], [("bank", ob)],
                        ident=self.identf[0:ncols, 0:ncols])

    def softmax_pair(self, QT, KT, k_pr, q_pr, qb, kbs, mask, mshift, heads, Vres, SB4, T32, PW, maskfn=None, extra=None,
                     mask_key="cm", cshift=3, copy_eng="act"):
        banks, ident = self.banks, self.ident
        n = len(kbs)
        qsl = slice(qb * 512, (qb + 1) * 512)

        def A(i):
            kb = kbs[i]
            ksl = slice(kb * 128, (kb + 1) * 128)
            delta = 512 * qb - 128 * kb
            moff = maskfn(delta) if maskfn is not None else (delta + mshift if -384 <= delta <= 0 else None)
            ex = extra(kb) if extra is not None else None
            for h, H in enumerate(heads):
                bank, bp = SB4[i % len(SB4)][h], H["bp"]
                self.mm(banks[bank][:], KT[bp:bp + 64, k_pr, ksl], QT[bp:bp + 64, q_pr, qsl], True, moff is None and ex is None,
                        [(id(KT), k_pr, kb // 4), (id(QT), q_pr, qb)], [("bank", bank)])
            for h, H in enumerate(heads):
                bank = SB4[i % len(SB4)][h]
                if ex is not None:
                    self.mm(banks[bank][:], ex[0], ex[1], False, moff is None, ex[2], [("bank", bank)])
                if moff is not None:
                    self.mm(banks[bank][:], ident[:], mask[:, moff:moff + 512], False, True, ["ident", mask_key], [("bank", bank)])

        def B(i):
            kb = kbs[i]
            ci = (512 * qb - 128 * kb) // 128 + cshift
            for h, H in enumerate(heads):
                bank = SB4[i % len(SB4)][h]
                self.tt("dve", T32[i % 2][h][:], banks[bank][:], H["abase"], ALU.add,
                        [("bank", bank), H["akeys"][1]], [("T32", i % 2, h)])
            for h, H in enumerate(heads):
                self.act(PW[i % 3][h][:], T32[i % 2][h][:], AF.Exp, [("T32", i % 2, h), H["akeys"][0]], [("PW", i % 3, h)],
                         bias=H["acst"][:, ci:ci + 1])

        def C(i):
            kb = kbs[i]
            for h, H in enumerate(heads):
                for (vf, ncols, otb, doff) in H["pvT"]:
                    self.mm(banks[otb][0:ncols, :], vf(kb), PW[i % 3][h][:], i == 0, i == n - 1,
                            [("PW", i % 3, h), (id(Vres), kb)], [("bank", otb)])

        self.run_stages(n, [A, B, C])
        for H in heads:
            self.untranspose_multi([(otb, ncols, [(ob, c0 + doff) for (ob, c0) in H["oacc"]]) for (vf, ncols, otb, doff) in H["pvT"]],
                                   copy_eng)


    def softmax_block(self, QT, KT, pr, bp, qb, kbs, mask, mshift, abase, acst, cshift, Vres, vfn, dvp, oacc, starts, SB, T32, PW,
                      first=True, maskfn=None, extra=None, qt_pr=None, mask_key="cm",
                      delta_of=None, ci_of=None, kkey_of=None, vkey_of=None, acst_key="acst", abase_key="abase", pvT=None):
        banks, ident = self.banks, self.ident
        n = len(kbs)
        qsl = slice(qb * 512, (qb + 1) * 512)
        if qt_pr is None:
            qt_pr = pr
        kd = 128 if bp is None else 64
        b0 = 0 if bp is None else bp
        if delta_of is None:
            delta_of = lambda kb: 512 * qb - 128 * kb
        if ci_of is None:
            ci_of = lambda delta: delta // 128 + cshift
        if kkey_of is None:
            kkey_of = lambda kb: (id(KT), pr, kb // 4)
        if vkey_of is None:
            vkey_of = lambda kb: (id(Vres), kb)

        def A(i):
            kb = kbs[i]
            bank = SB[i % 2]
            ksl = slice(kb * 128, (kb + 1) * 128)
            delta = delta_of(kb)
            moff = maskfn(delta) if maskfn is not None else (delta + mshift if -384 <= delta <= 0 else None)
            ex = extra(kb) if extra is not None else None
            self.mm(banks[bank][:], KT[b0:b0 + kd, pr, ksl], QT[b0:b0 + kd, qt_pr, qsl], True, moff is None and ex is None,
                    [kkey_of(kb), (id(QT), qt_pr, qb)], [("bank", bank)])
            if ex is not None:
                self.mm(banks[bank][:], ex[0], ex[1], False, moff is None, ex[2], [("bank", bank)])
            if moff is not None:
                self.mm(banks[bank][:], ident[:], mask[:, moff:moff + 512], False, True, ["ident", mask_key], [("bank", bank)])

        def B(i):
            kb = kbs[i]
            bank = SB[i % 2]
            ci = ci_of(delta_of(kb))
            self.stt(T32[i % 2][:], banks[bank][:], acst[:, ci:ci + 1], abase, ALU.add, ALU.add,
                     [("bank", bank), acst_key, abase_key], [("T32", i % 2, 0)])
            self.act(PW[i % 3][:], T32[i % 2][:], AF.Exp, [("T32", i % 2, 0)], [("PW", i % 3, 0)])

        def C(i):
            kb = kbs[i]
            pw = PW[i % 3]
            if pvT is not None:
                for (vf, ncols, otb, doff) in pvT:
                    self.mm(banks[otb][0:ncols, :], vf(kb), pw[:], i == 0, i == n - 1,
                            [("PW", i % 3, 0), vkey_of(kb)], [("bank", otb)])
                return
            for sub in range(4):
                ob, c0 = oacc[sub]
                self.mm(banks[ob][:, c0:c0 + dvp], pw[:, sub * 128:(sub + 1) * 128], vfn(kb),
                        first and i == 0 and sub in starts, i == n - 1,
                        [("PW", i % 3, 0), vkey_of(kb)], [("bank", ob)], skip=True)

        self.run_stages(n, [A, B, C])
        if pvT is not None:
            for (vf, ncols, otb, doff) in pvT:
                self.untranspose(otb, ncols, [(ob, c0 + doff) for (ob, c0) in oacc], "act")


    def l3_pair(self, p, qb, QT, KT, V, ZS, cms, tri, onesn, E32, SP, MACC, EC, AW):
        banks, ident = self.banks, self.ident
        BA = (0, 1)
        BB = ((2, 3), (4, 5))
        OT = (6, 7)
        kbs = list(range(4 * qb + 3, -1, -1))
        n = len(kbs)
        qsl = slice(qb * 512, (qb + 1) * 512)

        def A(i):
            kb = kbs[i]
            ksl = slice(kb * 128, (kb + 1) * 128)
            j = kb - 4 * qb
            for h in range(2):
                bp = 64 * h
                self.mm(banks[BA[h]][:], KT[bp:bp + 64, p, ksl], QT[bp:bp + 64, p, qsl], True, j < 0,
                        [(id(KT), p, kb // 4), (id(QT), p, qb)], [("bank", BA[h])])
            if j >= 0:
                off = 384 - 128 * j
                for h in range(2):
                    self.mm(banks[BA[h]][:], ident[:], cms[:, off:off + 512], False, True, ["ident", "cms"], [("bank", BA[h])])

        def B(i):
            for h in range(2):
                self.act(E32[h][i % 3], banks[BA[h]][:], AF.Exp, [("bank", BA[h])], [("E32", h, i % 3)])
            for h in range(2):
                self.act(SP[h][i % 2], E32[h][i % 3], AF.Ln, [("E32", h, i % 3)], [("SP", h, i % 2)], bias=1.0)
            for h in range(2):
                bb = BB[i % 2][h]
                self.mm(banks[bb][:], tri[:], SP[h][i % 2], True, i == 0, ["tri", ("SP", h, i % 2)], [("bank", bb)])
                if i > 0:
                    self.mm(banks[bb][:], onesn[:], MACC[h][(i - 1) % 2][:], False, True,
                            ["onesn", ("MACC", h, (i - 1) % 2)], [("bank", bb)])
            if i < n - 1:
                for h in range(2):
                    if i == 0:
                        self.cp("dve", MACC[h][0][:], SP[h][0], [("SP", h, 0)], [("MACC", h, 0)])
                    else:
                        self.tt("dve", MACC[h][i % 2][:], MACC[h][(i - 1) % 2][:], SP[h][i % 2], ALU.add,
                                [("SP", h, i % 2), ("MACC", h, (i - 1) % 2)], [("MACC", h, i % 2)])

        def B2(i):
            for h in range(2):
                bb = BB[i % 2][h]
                self.act(EC[h][i % 2][:], banks[bb][:], AF.Exp, [("bank", bb)], [("EC", h, i % 2)])
            for h in range(2):
                self.tt("dve", AW[h][i % 2][:], E32[h][i % 3], EC[h][i % 2][:], ALU.mult,
                        [("E32", h, i % 3), ("EC", h, i % 2)], [("AW", h, i % 2)])

        def C(i):
            kb = kbs[i]
            for h in range(2):
                hd = 2 * p + h
                self.mm(banks[OT[h]][0:64, :], V[:, kb, hd * 64:(hd + 1) * 64], AW[h][i % 2][:], i == 0, i == n - 1,
                        [("AW", h, i % 2), ("V", kb)], [("bank", OT[h])])

        for step in range(n + 3):
            if 0 <= step - 1 < n:
                B(step - 1)
            if step < n:
                A(step)
            if 0 <= step - 2 < n:
                B2(step - 2)
            if 0 <= step - 3 < n:
                C(step - 3)
        for h in range(2):
            hd = 2 * p + h
            self.untranspose_multi([(OT[h], 64, [(OT[h], 64 * k) for k in range(4)])], "dve")
            for sub in range(4):
                tq = qb * 4 + sub
                zs = ZS[:, tq, hd * 64:(hd + 1) * 64]
                self.tt("dve", zs, banks[OT[h]][:, sub * 64:(sub + 1) * 64], zs, ALU.mult, [("bank", OT[h]), ("ZS", tq)], [("ZS", tq)])


    def l3_block(self, hd, p, bp, qb, ob, QT, KT, V, ZS, cms, tri, onesn, E32, SP, MACC, AW, BA, BB, EC):
        banks, ident = self.banks, self.ident
        kbs = list(range(4 * qb + 3, -1, -1))
        n = len(kbs)
        qsl = slice(qb * 512, (qb + 1) * 512)
        qkeys = [(id(QT), p, qb)]

        def scores(bank, kb, last_extra):
            ksl = slice(kb * 128, (kb + 1) * 128)
            j = kb - 4 * qb
            diag = j >= 0
            self.mm(banks[bank][:], KT[bp:bp + 64, p, ksl], QT[bp:bp + 64, p, qsl], True, (not diag and not last_extra),
                    [(id(KT), p, kb // 4)] + qkeys, [("bank", bank)])
            if diag:
                off = 384 - 128 * j
                self.mm(banks[bank][:], ident[:], cms[:, off:off + 512], False, not last_extra, ["ident", "cms"], [("bank", bank)])

        def A(i):
            scores(BA[i % 2], kbs[i], False)

        def B(i):
            kb = kbs[i]
            ba, bb = BA[i % 2], BB[i % 2]
            e, sp = E32[i % 2], SP[i % 3]
            self.act(e[:], banks[ba][:], AF.Exp, [("bank", ba)], [("E32", i % 2)])
            self.act(sp[:], e[:], AF.Ln, [("E32", i % 2)], [("SP", i % 3)], bias=1.0)
            self.mm(banks[bb][:], tri[:], sp[:], True, i == 0, ["tri", ("SP", i % 3)], [("bank", bb)])
            if i > 0:
                self.mm(banks[bb][:], onesn[:], MACC[(i - 1) % 2][:], False, True, ["onesn", ("MACC", (i - 1) % 2)], [("bank", bb)])
            if i < n - 1:
                if i == 0:
                    self.cp("dve", MACC[0][:], sp[:], [("SP", i % 3)], [("MACC", 0)])
                else:
                    self.tt("dve", MACC[i % 2][:], MACC[(i - 1) % 2][:], sp[:], ALU.add,
                            [("SP", i % 3), ("MACC", (i - 1) % 2)], [("MACC", i % 2)])

        def B2(i):
            bb = BB[i % 2]
            ec = EC[i % 2]
            self.act(ec[:], banks[bb][:], AF.Exp, [("bank", bb)], [("EC", i % 2)])
            self.tt("dve", AW[i % 2][:], E32[i % 2][:], ec[:], ALU.mult, [("E32", i % 2), ("EC", i % 2)], [("AW", i % 2)])

        def C(i):
            kb = kbs[i]
            aw = AW[i % 2]
            self.mm(banks[otb][0:64, :], V[:, kb, hd * 64:(hd + 1) * 64], aw[:], i == 0, i == n - 1,
                    [("AW", i % 2), ("V", kb)], [("bank", otb)])

        otb = self.l3_ot[(hd * 64 + qb) % 2]
        self.run_stages(n, [A, B, B2, C])
        self.untranspose(otb, 64, [(ob, 64 * k) for k in range(4)], "dve")
        for sub in range(4):
            tq = qb * 4 + sub
            zs = ZS[:, tq, hd * 64:(hd + 1) * 64]
            self.tt("dve", zs, banks[ob][:, sub * 64:(sub + 1) * 64], zs, ALU.mult, [("bank", ob), ("ZS", tq)], [("ZS", tq)])


def chunked(w):
    return np.ascontiguousarray(w.reshape(8, 128, w.shape[1]).transpose(1, 0, 2))


def rep128(v):
    return np.ascontiguousarray(np.broadcast_to(v[None, :], (128, v.shape[0]))).astype(np.float32)


_PROGS = {}


def get_prog(L, S, final=False, T=None, res=True, fused=False):
    key = (L, S, final, T, res, fused)
    if key not in _PROGS:
        _PROGS[key] = LayerProg(L, S, final=final, T=T, res=res, fused=fused)
    return _PROGS[key]


def consts_common():
    return {"ident": np.eye(128, dtype=np.float32).astype(NPBF), "identf": np.eye(128, dtype=np.float32)}


def layer_inputs(L, S, g, h, gpre, P, res=None):
    m = consts_common()
    m["h_in"] = None if h is None else np.ascontiguousarray(h, dtype=np.float32)
    m["gpre"] = rep128(gpre)
    if res is not None:
        ogT, wout, gpost = res
        m["ogT"] = ogT
        m["wout"] = chunked(wout)
        m["gpost"] = rep128(gpost)
    if L == 0:
        w = P["a_w_in"]
        NSEL, NT = S // 64, S // 128
        NC = S // 16
        NCT = max(1, NC // 128)
        NCP = NCT * 128
        NCMP = NC - 1
        hk = g
        c64 = lambda o: w[:, o + hk * 64:o + hk * 64 + 64]
        glc = np.concatenate([w[:, 2560 + br * 16 + 4 * hk:2560 + br * 16 + 4 * hk + 4] for br in range(3)], axis=1)
        m["win"] = chunked(np.concatenate([w[:, 256 * hk:256 * hk + 256], c64(1024), c64(1024), c64(1280), c64(1280),
                                           c64(1536), c64(1536), c64(2048), c64(2048), c64(1792), c64(2304), glc,
                                           w[:, 2608 + 256 * hk:2608 + 256 * hk + 256]], axis=1))
        for kv in "kv":
            m["w1" + kv] = np.ascontiguousarray(P["a_cmp_w1_" + kv].reshape(16, 128, 256).transpose(1, 0, 2))
            pos = P["a_cmp_pos_" + kv]
            m["pos2" + kv] = np.ascontiguousarray(pos.reshape(16, 2, 64).transpose(1, 2, 0).reshape(128, 1, 16))
        w2k = P["a_cmp_w2_k"].reshape(2, 128, 64).transpose(1, 0, 2)
        m["w2k"] = np.ascontiguousarray(np.concatenate([w2k, w2k], axis=2))
        m["w2v"] = np.ascontiguousarray(P["a_cmp_w2_v"].reshape(2, 128, 64).transpose(1, 0, 2))
        cidx = 16 * np.arange(NCMP)[:, None] + np.arange(32)[None, :]
        ov = np.zeros((NCP, NSEL + 1), np.float32)
        np.add.at(ov, (np.repeat(np.arange(NCMP), 32), (cidx // 64).ravel()), 1.0 / 32)
        ov[:, NSEL] = 1.0
        m["ovaug"] = np.ascontiguousarray(ov.reshape(NCT, 128, NSEL + 1).transpose(1, 0, 2)).astype(NPBF)
        m["cm"] = band_strip(896, 384, lambda d: d >= 0)
        m["wm"] = band_strip(1408, 384, lambda d: (d >= 0) & (d < 512))
        nn = np.arange(128)[:, None]
        xx = np.arange(2560)[None, :]
        m["cmk"] = np.where(xx - 16 * nn - 31 >= 0, 0.0, NEG).astype(NPBF)
        jj = np.arange(128)[:, None, None]
        kb_ = np.arange(NT)[None, :, None]
        kk_ = np.arange(128)[None, None, :]
        m["EM"] = np.where(jj == 2 * kb_ + kk_ // 64, 30000.0, 0.0).astype(NPBF)
        sl = alibi(16)[4 * hk:4 * hk + 4]
        kk = np.arange(128)[:, None]
        qq = np.arange(512)[None, :]
        m["abaseA"] = np.ascontiguousarray(np.stack([-s_ * (qq - kk) for s_ in sl], axis=1).astype(np.float32))
        m["abaseC"] = np.ascontiguousarray(np.stack([-s_ * (qq - 16 * kk) for s_ in sl], axis=1).astype(np.float32))
        ND = NT + 3
        dl = (np.arange(ND) - 3) * 128.0
        m["acstA"] = np.ascontiguousarray(np.broadcast_to(np.stack([-s_ * dl for s_ in sl], 0)[None], (128, 4, ND)).astype(np.float32))
        dc = np.arange(16) * 512.0 - 31.0
        m["acstC"] = np.ascontiguousarray(np.broadcast_to(np.stack([-s_ * dc for s_ in sl], 0)[None], (128, 4, 16)).astype(np.float32))
        qi = np.arange(128)[:, None]
        xa = np.arange(2 * NSEL)[None, :]
        dd = xa - (NSEL - 1) - qi // 64
        m["adj"] = np.where((dd == 0) | (dd == -1), 1000.0, np.where(dd > 0, -1.0, 0.0)).astype(np.float32)
    if L == 1:
        w = P["b_w_in"]
        cs = slice(256 * g, 256 * g + 256)
        m["win"] = chunked(np.concatenate([w[:, 0:1024][:, cs], w[:, 1024:2048][:, cs], w[:, 2048:3072][:, cs],
                                           w[:, 3072:4096][:, cs]], axis=1))
        m["cm"] = band_strip(896, 384, lambda d: d >= 0)
        sl = alibi(8)[2 * g:2 * g + 2]
        kk = np.arange(128)[:, None]
        qq = np.arange(512)[None, :]
        m["abase"] = np.ascontiguousarray(np.stack([-s_ * (qq - kk) for s_ in sl], axis=1).astype(np.float32))
        ND = S // 128 + 3
        dl = (np.arange(ND) - 3) * 128.0
        m["acst"] = np.ascontiguousarray(np.broadcast_to(np.stack([-s_ * dl for s_ in sl], 0)[None], (128, 2, ND)).astype(np.float32))
        m["lamr"] = np.ascontiguousarray(np.broadcast_to(P["b_lambda"][None], (128, 4, 64)).astype(np.float32))
        m["sgbc"] = rep128(P["b_sub_gain"])
    if L == 2:
        w = P["c_w_in"]
        PAT = ((128, 1), (512, 4), (2048, 16))
        qc = lambda gg: w[:, gg * 256 + g * 64: gg * 256 + g * 64 + 64]
        kc = lambda gg: w[:, 768 + gg * 256 + g * 64: 768 + gg * 256 + g * 64 + 64]
        m["win"] = chunked(np.concatenate([qc(0), qc(1), qc(2), kc(0), kc(1), kc(2), w[:, 1536 + g * 256:1536 + (g + 1) * 256],
                                           w[:, 2560 + g * 256:2560 + (g + 1) * 256]], axis=1))
        sl = alibi(12).reshape(3, 4)[:, g]
        kk = np.arange(128)[:, None]
        qq = np.arange(512)[None, :]
        m["abase"] = np.ascontiguousarray(np.stack([-s_ * (qq - kk) for s_ in sl], axis=1).astype(np.float32))
        dl = (np.arange(20) - 3) * 128.0
        m["acst"] = np.ascontiguousarray(np.broadcast_to(np.stack([-s_ * dl for s_ in sl], 0)[None], (128, 3, 20)).astype(np.float32))
        for gg, (wd, dil) in enumerate(PAT):
            m["dm%d" % gg] = band_strip(384 + wd + 512, 384, lambda d, wd=wd, dil=dil: (d >= 0) & (d <= wd) & (d % dil == 0))
    if L == 3:
        w = P["d_w_in"]
        cs = slice(256 * g, 256 * g + 256)
        m["win"] = chunked(np.concatenate([w[:, 0:1024][:, cs], w[:, 1024:2048][:, cs], w[:, 2048:3072][:, cs],
                                           w[:, 3072:4096][:, cs]], axis=1))
        m["cms"] = band_strip(896, 384, lambda d: d > 0)
        jj = np.arange(128)[:, None]
        ss = np.arange(128)[None, :]
        m["tri"] = np.where(jj >= ss, -1.0, 0.0).astype(NPBF)
        m["onesn"] = np.full((128, 128), -1.0, np.float32).astype(NPBF)
    return m


def kernel_unfused(x, norm_pre, norm_post, a_w_in, a_w_out, a_cmp_pos_k, a_cmp_pos_v, a_cmp_w1_k, a_cmp_w2_k,
           a_cmp_w1_v, a_cmp_w2_v, b_w_in, b_w_out, b_lambda, b_sub_gain, c_w_in, c_w_out, d_w_in, d_w_out):
    f = lambda a: np.asarray(a, dtype=np.float32)
    x, norm_pre, norm_post = f(x), f(norm_pre), f(norm_post)
    B, S, _ = x.shape
    P = {"a_w_in": f(a_w_in)[0], "a_cmp_pos_k": f(a_cmp_pos_k)[0], "a_cmp_pos_v": f(a_cmp_pos_v)[0],
         "a_cmp_w1_k": f(a_cmp_w1_k)[0], "a_cmp_w2_k": f(a_cmp_w2_k)[0], "a_cmp_w1_v": f(a_cmp_w1_v)[0],
         "a_cmp_w2_v": f(a_cmp_w2_v)[0], "b_w_in": f(b_w_in)[0], "b_lambda": f(b_lambda)[0],
         "b_sub_gain": f(b_sub_gain)[0], "c_w_in": f(c_w_in)[0], "d_w_in": f(d_w_in)[0]}
    wouts = [f(a_w_out)[0], f(b_w_out)[0], f(c_w_out)[0], f(d_w_out)[0]]
    h = [x[b] for b in range(B)]
    ogT = None
    cores = list(range(8))
    for L in range(4):
        prog = get_prog(L, S, res=(L > 0))
        maps = []
        for core in cores:
            b, g = divmod(core, 4)
            res = None if L == 0 else (ogT[b], wouts[L - 1], norm_post[L - 1])
            maps.append(layer_inputs(L, S, g, h[b], norm_pre[L], P, res))
        r = run_bass_kernel_spmd(prog.nc, maps, core_ids=cores).results
        if L > 0:
            h = [np.asarray(r[4 * b]["h_out"], dtype=np.float32) for b in range(B)]
        ogT = [np.ascontiguousarray(np.concatenate([np.asarray(r[4 * b + g]["og"]) for g in range(4)], axis=1).T)
               for b in range(B)]
    T = S // 4
    prog = get_prog(4, S, final=True, T=T)
    maps = []
    for core in cores:
        b, q = divmod(core, 4)
        sl = slice(q * T, (q + 1) * T)
        m = consts_common()
        m["h_in"] = np.ascontiguousarray(h[b][sl])
        m["ogT"] = np.ascontiguousarray(ogT[b][:, sl])
        m["wout"] = chunked(wouts[3])
        m["gpost"] = rep128(norm_post[3])
        maps.append(m)
    r = run_bass_kernel_spmd(prog.nc, maps, core_ids=cores).results
    out = np.stack([np.concatenate([np.asarray(r[4 * b + q]["h_out"], dtype=np.float32) for q in range(4)], axis=0)
                    for b in range(B)])
    return out.astype(np.float32)


def kernel(x, norm_pre, norm_post, a_w_in, a_w_out, a_cmp_pos_k, a_cmp_pos_v, a_cmp_w1_k, a_cmp_w2_k,
           a_cmp_w1_v, a_cmp_w2_v, b_w_in, b_w_out, b_lambda, b_sub_gain, c_w_in, c_w_out, d_w_in, d_w_out):
    f = lambda a: np.asarray(a, dtype=np.float32)
    x, norm_pre, norm_post = f(x), f(norm_pre), f(norm_post)
    B, S, _ = x.shape
    P = {"a_w_in": f(a_w_in)[0], "a_cmp_pos_k": f(a_cmp_pos_k)[0], "a_cmp_pos_v": f(a_cmp_pos_v)[0],
         "a_cmp_w1_k": f(a_cmp_w1_k)[0], "a_cmp_w2_k": f(a_cmp_w2_k)[0], "a_cmp_w1_v": f(a_cmp_w1_v)[0],
         "a_cmp_w2_v": f(a_cmp_w2_v)[0], "b_w_in": f(b_w_in)[0], "b_lambda": f(b_lambda)[0],
         "b_sub_gain": f(b_sub_gain)[0], "c_w_in": f(c_w_in)[0], "d_w_in": f(d_w_in)[0]}
    wouts = [f(a_w_out)[0], f(b_w_out)[0], f(c_w_out)[0], f(d_w_out)[0]]
    prog = get_prog(0, S, fused=True)
    cores = list(range(8))
    maps = []
    for core in cores:
        b, g = divmod(core, 4)
        m = dict(consts_common())
        m["x"] = np.ascontiguousarray(x[b])
        for L in range(4):
            lm = layer_inputs(L, S, g, None, norm_pre[L], P, None)
            lm.pop("h_in")
            lm.pop("ident")
            lm.pop("identf")
            if L > 0:
                lm["wout"] = chunked(wouts[L - 1])
                lm["gpost"] = rep128(norm_post[L - 1])
            for k, v in lm.items():
                m["L%d_%s" % (L, k)] = v
        m["F_wout"] = chunked(wouts[3])
        m["F_gpost"] = rep128(norm_post[3])
        maps.append(m)
    r = run_bass_kernel_spmd(prog.nc, maps, core_ids=cores).results
    return np.stack([np.asarray(r[4 * b]["out"], dtype=np.float32) for b in range(B)]).astype(np.float32)
```
